# Optimizing a Trainium2 kernel written in Bass

```python
import jax, jax.numpy as jnp
from jax import lax
import numpy as np

D_MODEL = 2048
BATCH = 8
SEQ = 4096
DEPTH = 4

N_MIXERS = 3
GRID_W = 64
ROPE_THETA = 500000.0
NORM_EPS = 1e-6
D_FF = -(-8 * D_MODEL // (3 * 256)) * 256
N_A = (DEPTH + 2) // 3
N_B = (DEPTH + 1) // 3
N_C = DEPTH // 3

A_HEADS = 16
A_KV_HEADS = 4
A_HEAD_DIM = D_MODEL // A_HEADS
A_ROT_DIM = A_HEAD_DIM // 4
WINDOW = 128
A_BLOCK = 128

B_HEADS = 16
B_NOPE_DIM = 128
B_ROPE_DIM = 64
B_V_DIM = 128
B_Q_RANK = 512
B_KV_RANK = 512
B_QBLOCK = 128

C_HEADS = 16
C_HEAD_DIM = D_MODEL // C_HEADS
NB_H_MAX = 8
NB_W = 16
C_QCOLS = 16
C_KCOLS = C_QCOLS + NB_W
N_CBLK = GRID_W // C_QCOLS

kernel_name = "hybrid_swa_mla_natten_encoder"


def rms_norm(x, g):
    xf = x.astype(jnp.float32)
    y = xf * lax.rsqrt(jnp.mean(xf * xf, axis=-1, keepdims=True) + NORM_EPS)
    return (y * g.astype(jnp.float32)).astype(x.dtype)


def rope_tables(seq, dim):
    pos = jnp.arange(seq, dtype=jnp.float32)
    inv = ROPE_THETA ** (-jnp.arange(0, dim, 2, dtype=jnp.float32) / dim)
    ang = pos[:, None] * inv[None, :]
    return jnp.cos(ang), jnp.sin(ang)


def apply_rope(x, cos, sin):
    half = x.shape[-1] // 2
    x1 = x[..., :half].astype(jnp.float32)
    x2 = x[..., half:].astype(jnp.float32)
    c, s = cos[:, None, :], sin[:, None, :]
    return jnp.concatenate([x1 * c - x2 * s, x2 * c + x1 * s], axis=-1).astype(x.dtype)


def window_gqa(x, w_qkv, sinks, w_o, cos, sin):
    B, S, _ = x.shape
    nb = S // A_BLOCK
    g = A_HEADS // A_KV_HEADS
    qkv = x @ w_qkv
    q, k, v = jnp.split(qkv, [A_HEADS * A_HEAD_DIM, (A_HEADS + A_KV_HEADS) * A_HEAD_DIM], axis=-1)
    q = q.reshape(B, S, A_HEADS, A_HEAD_DIM)
    k = k.reshape(B, S, A_KV_HEADS, A_HEAD_DIM)
    v = v.reshape(B, S, A_KV_HEADS, A_HEAD_DIM)
    q = jnp.concatenate([apply_rope(q[..., :A_ROT_DIM], cos, sin), q[..., A_ROT_DIM:]], axis=-1)
    k = jnp.concatenate([apply_rope(k[..., :A_ROT_DIM], cos, sin), k[..., A_ROT_DIM:]], axis=-1)
    qb = q.reshape(B, nb, A_BLOCK, A_KV_HEADS, g, A_HEAD_DIM)
    pad = ((0, 0), (A_BLOCK, A_BLOCK), (0, 0), (0, 0))
    kp = jnp.pad(k, pad).reshape(B, nb + 2, A_BLOCK, A_KV_HEADS, A_HEAD_DIM)
    vp = jnp.pad(v, pad).reshape(B, nb + 2, A_BLOCK, A_KV_HEADS, A_HEAD_DIM)
    kw = jnp.concatenate([kp[:, :-2], kp[:, 1:-1], kp[:, 2:]], axis=2)
    vw = jnp.concatenate([vp[:, :-2], vp[:, 1:-1], vp[:, 2:]], axis=2)
    a = np.arange(A_BLOCK)[:, None]
    j = np.arange(3 * A_BLOCK)[None, :]
    band = np.abs(j - A_BLOCK - a) <= WINDOW
    kabs = (np.arange(nb)[:, None] - 1) * A_BLOCK + np.arange(3 * A_BLOCK)[None, :]
    valid = band[None] & ((kabs >= 0) & (kabs < S))[:, None, :]
    scale = A_HEAD_DIM ** -0.5
    s = jnp.einsum('bnqkgd,bnjkd->bnkgqj', qb, kw).astype(jnp.float32) * scale
    s = jnp.where(valid[None, :, None, None, :, :], s, -jnp.inf)
    sink = sinks.astype(jnp.float32).reshape(A_KV_HEADS, g)[None, None, :, :, None, None]
    m = jnp.maximum(jnp.max(s, axis=-1, keepdims=True), sink)
    p = jnp.exp(s - m)
    probs = p / (jnp.sum(p, axis=-1, keepdims=True) + jnp.exp(sink - m))
    o = jnp.einsum('bnkgqj,bnjkd->bnqkgd', probs.astype(v.dtype), vw)
    return o.reshape(B, S, A_HEADS * A_HEAD_DIM) @ w_o


def latent_attention(x, w_down, q_norm, w_uq, kv_norm, w_ukv, w_o, cos, sin):
    B, S, _ = x.shape
    nb = S // B_QBLOCK
    c = x @ w_down
    cq, ckv, k_rope = jnp.split(c, [B_Q_RANK, B_Q_RANK + B_KV_RANK], axis=-1)
    cq = rms_norm(cq, q_norm)
    ckv = rms_norm(ckv, kv_norm)
    q = (cq @ w_uq).reshape(B, S, B_HEADS, B_NOPE_DIM + B_ROPE_DIM)
    q_nope, q_rope = q[..., :B_NOPE_DIM], apply_rope(q[..., B_NOPE_DIM:], cos, sin)
    k_rope = apply_rope(k_rope[:, :, None, :], cos, sin)[:, :, 0, :]
    kv = (ckv @ w_ukv).reshape(B, S, B_HEADS, B_NOPE_DIM + B_V_DIM)
    k_nope, v = kv[..., :B_NOPE_DIM], kv[..., B_NOPE_DIM:]
    scale = (B_NOPE_DIM + B_ROPE_DIM) ** -0.5
    qn_blocks = q_nope.reshape(B, nb, B_QBLOCK, B_HEADS, B_NOPE_DIM).transpose(1, 0, 2, 3, 4)
    qr_blocks = q_rope.reshape(B, nb, B_QBLOCK, B_HEADS, B_ROPE_DIM).transpose(1, 0, 2, 3, 4)

    def attend(blk):
        qn, qr = blk
        s = (jnp.einsum('bqhd,bkhd->bhqk', qn, k_nope)
             + jnp.einsum('bqhr,bkr->bhqk', qr, k_rope)).astype(jnp.float32) * scale
        p = jax.nn.softmax(s, axis=-1)
        return jnp.einsum('bhqk,bkhd->bqhd', p.astype(v.dtype), v)

    o = lax.map(attend, (qn_blocks, qr_blocks))
    o = o.transpose(1, 0, 2, 3, 4).reshape(B, S, B_HEADS * B_V_DIM)
    return o @ w_o


def neighborhood_attention(x, w_qkv, rel_bias, w_o):
    B, S, _ = x.shape
    rows = S // GRID_W
    wh = min(NB_H_MAX, rows)
    qkv = (x @ w_qkv).reshape(B, rows, GRID_W, 3, C_HEADS, C_HEAD_DIM)
    q, k, v = qkv[:, :, :, 0], qkv[:, :, :, 1], qkv[:, :, :, 2]
    qcol = np.arange(GRID_W).reshape(N_CBLK, C_QCOLS)
    kstart = np.clip(np.arange(N_CBLK) * C_QCOLS - NB_W // 2, 0, GRID_W - C_KCOLS)
    kcol = kstart[:, None] + np.arange(C_KCOLS)[None, :]
    cstart = np.clip(qcol - NB_W // 2, 0, GRID_W - NB_W)
    col_ok = (kcol[:, None, :] >= cstart[:, :, None]) & (kcol[:, None, :] < cstart[:, :, None] + NB_W)
    dcol_idx = np.clip(kcol[:, None, :] - qcol[:, :, None] + NB_W - 1, 0, 2 * NB_W - 2)
    bias_c = rel_bias[:, :, dcol_idx]
    mask = np.broadcast_to(col_ok[:, :, None, :], (N_CBLK, C_QCOLS, wh, C_KCOLS)).reshape(
        N_CBLK, C_QCOLS, wh * C_KCOLS)
    scale = C_HEAD_DIM ** -0.5
    q_rows = q.transpose(1, 0, 2, 3, 4)

    def row_step(args):
        r, qr = args
        rs = jnp.clip(r - wh // 2, 0, rows - wh)
        k_rows = lax.dynamic_slice_in_dim(k, rs, wh, axis=1)
        v_rows = lax.dynamic_slice_in_dim(v, rs, wh, axis=1)
        kb = k_rows[:, :, kcol].transpose(0, 2, 1, 3, 4, 5).reshape(
            B, N_CBLK, wh * C_KCOLS, C_HEADS, C_HEAD_DIM)
        vb = v_rows[:, :, kcol].transpose(0, 2, 1, 3, 4, 5).reshape(
            B, N_CBLK, wh * C_KCOLS, C_HEADS, C_HEAD_DIM)
        qb = qr.reshape(B, N_CBLK, C_QCOLS, C_HEADS, C_HEAD_DIM)
        dr_idx = rs - r + jnp.arange(wh) + NB_H_MAX - 1
        bias = jnp.take(bias_c, dr_idx, axis=1).transpose(0, 2, 3, 1, 4).reshape(
            C_HEADS, N_CBLK, C_QCOLS, wh * C_KCOLS)
        s = jnp.einsum('bjqhd,bjkhd->bhjqk', qb, kb).astype(jnp.float32) * scale
        s = jnp.where(mask, s + bias.astype(jnp.float32), -jnp.inf)
        p = jax.nn.softmax(s, axis=-1)
        o = jnp.einsum('bhjqk,bjkhd->bjqhd', p.astype(vb.dtype), vb)
        return o.reshape(B, GRID_W, C_HEADS * C_HEAD_DIM)

    o = lax.map(row_step, (jnp.arange(rows, dtype=jnp.int32), q_rows))
    o = o.transpose(1, 0, 2, 3).reshape(B, S, C_HEADS * C_HEAD_DIM)
    return o @ w_o


def swiglu(h, w_in, w_out):
    gate, up = jnp.split(h @ w_in, 2, axis=-1)
    return (jax.nn.silu(gate) * up) @ w_out


def setup_inputs(seed: int = 0) -> dict:
    key = jax.random.key(seed)
    ks = jax.random.split(key, 24)
    f32 = jnp.float32

    def w(k, shape, fan_in):
        return jax.random.normal(k, shape, f32) * (fan_in ** -0.5)

    def gain(k, shape):
        return 1.0 + 0.02 * jax.random.normal(k, shape, f32)

    a_qkv_w = (A_HEADS + 2 * A_KV_HEADS) * A_HEAD_DIM
    return {
        "x": jax.random.normal(ks[0], (BATCH, SEQ, D_MODEL), f32),
        "pre_mix_norm": gain(ks[1], (DEPTH, D_MODEL)),
        "post_mix_norm": gain(ks[2], (DEPTH, D_MODEL)),
        "pre_ffn_norm": gain(ks[3], (DEPTH, D_MODEL)),
        "post_ffn_norm": gain(ks[4], (DEPTH, D_MODEL)),
        "a_w_qkv": w(ks[5], (N_A, D_MODEL, a_qkv_w), D_MODEL),
        "a_sinks": 0.5 * jax.random.normal(ks[6], (N_A, A_HEADS), f32),
        "a_w_o": w(ks[7], (N_A, A_HEADS * A_HEAD_DIM, D_MODEL), A_HEADS * A_HEAD_DIM),
        "b_w_down": w(ks[8], (N_B, D_MODEL, B_Q_RANK + B_KV_RANK + B_ROPE_DIM), D_MODEL),
        "b_q_norm": gain(ks[9], (N_B, B_Q_RANK)),
        "b_w_uq": w(ks[10], (N_B, B_Q_RANK, B_HEADS * (B_NOPE_DIM + B_ROPE_DIM)), B_Q_RANK),
        "b_kv_norm": gain(ks[11], (N_B, B_KV_RANK)),
        "b_w_ukv": w(ks[12], (N_B, B_KV_RANK, B_HEADS * (B_NOPE_DIM + B_V_DIM)), B_KV_RANK),
        "b_w_o": w(ks[13], (N_B, B_HEADS * B_V_DIM, D_MODEL), B_HEADS * B_V_DIM),
        "c_w_qkv": w(ks[14], (N_C, D_MODEL, 3 * C_HEADS * C_HEAD_DIM), D_MODEL),
        "c_rel_bias": 0.1 * jax.random.normal(ks[15], (N_C, C_HEADS, 2 * NB_H_MAX - 1, 2 * NB_W - 1), f32),
        "c_w_o": w(ks[16], (N_C, C_HEADS * C_HEAD_DIM, D_MODEL), C_HEADS * C_HEAD_DIM),
        "ffn_w_in": w(ks[17], (DEPTH, D_MODEL, 2 * D_FF), D_MODEL),
        "ffn_w_out": w(ks[18], (DEPTH, D_FF, D_MODEL), D_FF),
    }


def reference(x, pre_mix_norm, post_mix_norm, pre_ffn_norm, post_ffn_norm,
              a_w_qkv, a_sinks, a_w_o,
              b_w_down, b_q_norm, b_w_uq, b_kv_norm, b_w_ukv, b_w_o,
              c_w_qkv, c_rel_bias, c_w_o,
              ffn_w_in, ffn_w_out):
    S = x.shape[1]
    cos_a, sin_a = rope_tables(S, A_ROT_DIM)
    cos_b, sin_b = rope_tables(S, B_ROPE_DIM)
    for i in range(DEPTH):
        kind, slot = i % N_MIXERS, i // N_MIXERS
        h = rms_norm(x, pre_mix_norm[i])
        if kind == 0:
            m = window_gqa(h, a_w_qkv[slot], a_sinks[slot], a_w_o[slot], cos_a, sin_a)
        elif kind == 1:
            m = latent_attention(h, b_w_down[slot], b_q_norm[slot], b_w_uq[slot],
                                 b_kv_norm[slot], b_w_ukv[slot], b_w_o[slot], cos_b, sin_b)
        else:
            m = neighborhood_attention(h, c_w_qkv[slot], c_rel_bias[slot], c_w_o[slot])
        x = x + rms_norm(m, post_mix_norm[i])
        h = rms_norm(x, pre_ffn_norm[i])
        x = x + rms_norm(swiglu(h, ffn_w_in[i], ffn_w_out[i]), post_ffn_norm[i])
    return x
```

```python
import contextlib
import numpy as np
import ml_dtypes
import concourse.bass as bass
import concourse.mybir as mybir
from concourse.bass_utils import run_bass_kernel_spmd

F32 = mybir.dt.float32
BF16 = mybir.dt.bfloat16
ALU = mybir.AluOpType
AF = mybir.ActivationFunctionType
AX = mybir.AxisListType

D = 2048
S = 4096
DFF = 5632
NT = 512
NTT = S // NT
DC = D // 128
FC = DFF // 128
EPS = 1e-6
THETA = 500000.0
SLOT = 5632
NSLOT = 4
NEG = -30000.0


class Sem:
    def __init__(self, h, name):
        self.h = h
        self.name = name
        self.count = 0


class Dep:
    __slots__ = ("w", "r", "excl")

    def __init__(self, excl=False):
        self.w = None
        self.r = {}
        self.excl = excl


def deps(n):
    return [Dep() for _ in range(n)]


def pdeps(n):
    return [Dep(True) for _ in range(n)]


def PDep():
    return Dep(True)


class Q:
    def __init__(self, name, is_pe=False):
        self.name = name
        self.thunks = []
        self.sem = None
        self.seen = {}
        self.is_pe = is_pe


class MK:
    def __init__(self):
        self.nc = bass.Bass("TRN2", target_bir_lowering=False)
        self.stack = contextlib.ExitStack()
        self.q = {n: Q(n, n == "pe") for n in ("pe", "act", "dve", "pool", "sp")}
        for n in ("pe", "act", "dve", "pool"):
            self.q[n].sem = self.sem("e_" + n)
        self.dma_sems = []
        self.free_sems = []
        self.phase_sems = []
        self.n_inst = 0
        self.uid = 0

    def sem(self, name):
        h = self.stack.enter_context(self.nc.semaphore(name))
        return Sem(h, name)

    def dsem(self, name, phase=True):
        if self.free_sems:
            s = self.free_sems.pop()
        else:
            s = self.sem(f"d{len(self.dma_sems)}")
            self.dma_sems.append(s)
        if phase:
            self.phase_sems.append(s)
        return s

    def end_phase(self):
        self.barrier()
        self.free_sems.extend(self.phase_sems)
        self.phase_sems = []

    def release(self, sems):
        self.free_sems.extend(sems)

    def sbuf(self, name, shape, dt, stack=None):
        self.uid += 1
        return (stack or self.stack).enter_context(
            self.nc.sbuf_tensor(f"{name}_{self.uid}", list(shape), dt))

    def psum(self, name, shape, dt=F32, stack=None):
        self.uid += 1
        return (stack or self.stack).enter_context(
            self.nc.psum_tensor(f"{name}_{self.uid}", list(shape), dt))

    def dram(self, name, shape, dt, kind="Internal"):
        return self.nc.dram_tensor(name, list(shape), dt, kind=kind)

    def _waits(self, q, R, W):
        need = {}

        def add(ev):
            if ev is None:
                return
            s, v = ev
            if need.get(s, 0) < v:
                need[s] = v
        for d in R:
            add(d.w)
        for d in W:
            add(d.w)
            for s, v in d.r.items():
                add((s, v))
        for s, v in need.items():
            if q.is_pe and s is q.sem:
                continue
            if q.seen.get(s, 0) >= v:
                continue
            q.seen[s] = v
            q.thunks.append(("w", s.h, v))

    def _commit(self, ev, R, W):
        s, v = ev
        for d in W:
            d.w = ev
            d.r = {}
        for d in R:
            if d.r.get(s, 0) < v:
                d.r[s] = v

    def op(self, qn, fn, R=(), W=()):
        q = self.q[qn]
        if any(d.excl for d in R):
            W = list(W) + [d for d in R if d.excl]
            R = [d for d in R if not d.excl]
        self._waits(q, R, W)
        q.sem.count += 1
        ev = (q.sem, q.sem.count)
        q.thunks.append(("i", fn, q.sem.h, 1))
        self._commit(ev, R, W)
        self.n_inst += 1
        return ev

    def dma(self, qn, sem, out, in_, R=(), W=(), **kw):
        q = self.q[qn]
        self._waits(q, R, W)
        sem.count += 16
        ev = (sem, sem.count)
        q.thunks.append(("i", lambda e, o=out, i=in_, k=kw: e.dma_start(out=o, in_=i, **k), sem.h, 16))
        self._commit(ev, R, W)
        self.n_inst += 1
        return ev

    def barrier(self):
        allsems = [self.q[n].sem for n in ("pe", "act", "dve", "pool")] + self.dma_sems
        for qn, q in self.q.items():
            for s in allsems:
                if s.count == 0:
                    continue
                if q.is_pe and s is q.sem:
                    continue
                if q.seen.get(s, 0) >= s.count:
                    continue
                q.seen[s] = s.count
                q.thunks.append(("w", s.h, s.count))

    def finish(self):
        self.barrier()
        nc = self.nc
        engs = {"pe": "tensor", "act": "scalar", "dve": "vector", "pool": "gpsimd", "sp": "sync"}
        with nc.Block() as block:
            for qn, attr in engs.items():
                q = self.q[qn]

                def body(e, q=q):
                    for t in q.thunks:
                        if t[0] == "w":
                            e.wait_ge(t[1], t[2])
                        else:
                            t[1](e).then_inc(t[2], t[3])
                getattr(block, attr)(body)
        self.stack.close()
        return nc

    def mm(self, out, lhsT, rhs, start, stop, R, W):
        return self.op("pe", lambda e: e.matmul(out, lhsT, rhs, start=start, stop=stop), R=R, W=W)

    def act(self, out, in_, func, R, W, bias=None, scale=None):
        kw = {}
        if bias is not None:
            kw["bias"] = bias
        if scale is not None:
            kw["scale"] = scale
        return self.op("act", lambda e: e.activation(out=out, in_=in_, func=func, **kw), R=R, W=W)

    def tt(self, qn, out, in0, in1, op, R, W):
        return self.op(qn, lambda e: e.tensor_tensor(out=out, in0=in0, in1=in1, op=op), R=R, W=W)

    def ts(self, qn, out, in0, s1, s2, op0, op1, R, W):
        return self.op(qn, lambda e: e.tensor_scalar(out=out, in0=in0, scalar1=s1, scalar2=s2, op0=op0, op1=op1),
                       R=R, W=W)

    def stt(self, qn, out, in0, scalar, in1, op0, op1, R, W):
        return self.op(qn, lambda e: e.scalar_tensor_tensor(out=out, in0=in0, scalar=scalar, in1=in1,
                                                            op0=op0, op1=op1), R=R, W=W)


class WStream:
    def __init__(self, mk, stack, nslot=NSLOT):
        self.mk = mk
        self.slots = [mk.sbuf("wslot", [128, SLOT], BF16, stack) for _ in range(nslot)]
        self.deps = deps(nslot)
        self.sems = [mk.dsem(f"ws{mk.uid}_{i}") for i in range(nslot)]
        self.plan = []
        self.issued = 0
        self.taken = 0

    def schedule(self, tiles):
        self.plan.extend(tiles)

    def _issue_upto(self, n):
        n = min(n, len(self.plan))
        while self.issued < n:
            ap, E, rd = self.plan[self.issued]
            s = self.issued % len(self.slots)
            self.mk.dma("sp", self.sems[s], self.slots[s][:, 0:E], ap, R=list(rd), W=[self.deps[s]])
            self.issued += 1

    def get(self):
        i = self.taken
        self._issue_upto(i + len(self.slots))
        self.taken += 1
        s = i % len(self.slots)
        return self.slots[s], self.deps[s]

    def prefetch(self):
        self._issue_upto(self.taken + len(self.slots))


def tile_w(W, wt):
    K, N = W.shape
    KC, NTL = K // 128, N // wt
    return np.ascontiguousarray(W.reshape(KC, 128, NTL, wt).transpose(2, 1, 0, 3)).reshape(NTL, 128, KC * wt)


def rope_tabs(dim, nrep):
    pos = np.arange(S, dtype=np.float32)
    inv = (np.float32(THETA) ** (-np.arange(0, dim, 2, dtype=np.float32) / np.float32(dim))).astype(np.float32)
    ang = (pos[None, :] * inv[:, None]).astype(np.float32)
    c, s = np.cos(ang).astype(np.float32), np.sin(ang).astype(np.float32)
    cosT = np.ones((128, S), np.float32)
    sinT = np.zeros((128, S), np.float32)
    R = np.zeros((128, 128), np.float32)
    half = dim // 2
    for r in range(nrep):
        b = r * dim
        cosT[b:b + half] = c
        cosT[b + half:b + dim] = c
        sinT[b:b + half] = -s
        sinT[b + half:b + dim] = s
        for i in range(half):
            R[b + i + half, b + i] = 1.0
            R[b + i, b + i + half] = 1.0
    return cosT, sinT, R


def c_tile_plan():
    plan = []
    for p in range(32):
        if p == 0:
            plan.append(([0, 1, 2, 3], 5))
        elif p == 1:
            plan.append(([0, 1, 2, 3], 9))
        elif p == 30:
            plan.append(([28, 29, 30, 31], 13))
        elif p == 31:
            plan.append(([28, 29, 30, 31], 17))
        else:
            plan.append(([p - 2, p - 1, p, p + 1, p + 2], 0))
    return plan


def c_bias_tables(rel_bias):
    cases = [(10, 10 + o) for o in (-2, -1, 0, 1, 2)]
    for p in (0, 1):
        cases += [(p, kp) for kp in (0, 1, 2, 3)]
    for p in (30, 31):
        cases += [(p, kp) for kp in (28, 29, 30, 31)]
    kk = np.arange(128)
    kr_l, kc = kk // 64, kk % 64
    out = np.full((16, 128, 21, 128), NEG, np.float32)
    for ci, (p, kp) in enumerate(cases):
        qr = 2 * p + kr_l[None, :]
        qc = kc[None, :]
        kr = 2 * kp + kr_l[:, None]
        kcc = kc[:, None]
        rs = np.clip(qr - 4, 0, 56)
        rowv = (kr >= rs) & (kr < rs + 8)
        cs = np.clip(qc - 8, 0, 48)
        colv = (kcc >= cs) & (kcc < cs + 16)
        valid = rowv & colv
        dr = np.clip(kr - qr + 7, 0, 14)
        dc = np.clip(kcc - qc + 15, 0, 30)
        g = rel_bias[:, dr, dc]
        out[:, :, ci, :] = np.where(valid[None], g, np.float32(NEG))
    return out


class Prog:
    def __init__(self, kinds):
        self.kinds = kinds
        self.mk = MK()
        self.in_names = []

    def inp(self, name, shape, dt):
        self.in_names.append(name)
        return self.mk.dram(name, shape, dt, kind="ExternalInput")

    @staticmethod
    def wspecs(kind):
        if kind == 0:
            mix = [("qkv", 2048, 3072, 256), ("o", 2048, 2048, 256)]
        elif kind == 1:
            mix = [("down", 2048, 1152, 128), ("uq", 512, 3072, 1024), ("ukv", 512, 4096, 1024),
                   ("o", 2048, 2048, 256)]
        else:
            mix = [("qkv", 2048, 6144, 256), ("o", 2048, 2048, 256)]
        return mix + [("win", 2048, 2 * DFF, 256), ("wout", DFF, 2048, 128)]

    def build(self):
        mk = self.mk
        L = len(self.kinds)
        self.xT = self.inp("xT", [DC, 128, S], F32)
        self.yT = mk.dram("yT", [DC, 128, S], F32, kind="ExternalOutput")
        self.gains_d = self.inp("gains", [128, 4 * L * DC], F32)
        self.ones_d = self.inp("ones", [128, 5 * 128], BF16)
        self.w32, self.wbf, self.wdep, self.wshape = {}, {}, {}, {}
        for l, kind in enumerate(self.kinds):
            for key, K, N, wt in self.wspecs(kind):
                ntl, E = N // wt, (K // 128) * wt
                nm = f"l{l}_{key}"
                self.w32[nm] = self.inp(nm, [ntl, 128, E], F32)
                self.wbf[nm] = mk.dram(nm + "_bf", [ntl, 128, E], BF16)
                self.wdep[nm] = Dep()
                self.wshape[nm] = (ntl, E, wt, K // 128)
        if 0 in self.kinds:
            self.ropeA_d = self.inp("ropeA", [128, 2, S], F32)
            self.RA_d = self.inp("RA", [128, 128], BF16)
            self.maskA_d = self.inp("maskA", [128, 2, 512], BF16)
        if 1 in self.kinds:
            self.ropeB_d = self.inp("ropeB", [128, 2, S], F32)
            self.RB_d = self.inp("RB", [128, 128], BF16)
        self.aux = {}
        for l, kind in enumerate(self.kinds):
            if kind == 0:
                self.aux[l] = self.inp(f"l{l}_sinks", [128, 16], F32)
            elif kind == 1:
                self.aux[l] = self.inp(f"l{l}_qkvn", [128, 8], F32)
            else:
                self.aux[l] = self.inp(f"l{l}_cbias", [16, 128, 21 * 128], F32)
        self.qT = mk.dram("qT_s", [32, 128, S], BF16)
        self.kT = mk.dram("kT_s", [17, 128, S], BF16)
        self.vS = mk.dram("v_s", [16, 128, 32, 128], BF16)
        self.oT = mk.dram("oT_s", [16, 128, S], BF16)

        self.gains = mk.sbuf("gains", [128, 4 * L * DC], F32)
        self.ones = mk.sbuf("ones", [128, 5 * 128], BF16)
        self.cdep = Dep()
        csem = mk.dsem("const")
        mk.dma("sp", csem, self.gains[:], self.gains_d.ap(), W=[self.cdep])
        mk.dma("sp", csem, self.ones[:], self.ones_d.ap(), W=[self.cdep])
        self.one1 = self.ones[:, 0:128]
        self.oneD = self.ones[:, 128:256]
        self.one512 = self.ones[:, 256:384]
        self.oneH = self.ones[:, 384:512]
        self.oneL = self.ones[:, 512:640]
        if 0 in self.kinds:
            self.RA = mk.sbuf("RA", [128, 128], BF16)
            self.maskA = mk.sbuf("maskA", [128, 2, 512], BF16)
            mk.dma("sp", csem, self.RA[:], self.RA_d.ap(), W=[self.cdep])
            mk.dma("sp", csem, self.maskA[:], self.maskA_d.ap(), W=[self.cdep])
        if 1 in self.kinds:
            self.RB = mk.sbuf("RB", [128, 128], BF16)
            mk.dma("sp", csem, self.RB[:], self.RB_d.ap(), W=[self.cdep])
        self.stats = mk.sbuf("stats", [128, 64], F32)
        self.stats_dep = Dep()

        self.conv_sems = {}
        for cv in self.conv_list(0):
            self.emit_conv(cv)

        for l, kind in enumerate(self.kinds):
            xin = self.xT if l == 0 else self.yT
            p1, p2 = [(self.p1_A, self.p2_A), (self.p1_B, self.p2_B), (self.p1_C, self.p2_C)][kind]
            p1(l, xin)
            mk.end_phase()
            p2(l)
            mk.end_phase()
            self.p3(l, xin, self.conv_list(l + 1) if l + 1 < L else [])
            mk.end_phase()
            mk.release(self.conv_sems.pop(l))
        return mk.finish()

    def conv_list(self, l):
        mk = self.mk
        out = []
        sems = []
        for key, K, N, wt in self.wspecs(self.kinds[l]):
            nm = f"l{l}_{key}"
            ntl, E, _, _ = self.wshape[nm]
            r = 2048 if E % 2048 == 0 else 1408
            a = E // r
            src = self.w32[nm].ap().rearrange("t p (a r) -> (t p a) r", r=r)
            dst = self.wbf[nm].ap().rearrange("t p (a r) -> (t p a) r", r=r)
            rows = ntl * 128 * a
            sem = mk.dsem("cv", phase=False)
            sems.append(sem)
            step = 4096
            for r0 in range(0, rows, step):
                r1 = min(rows, r0 + step)
                out.append((sem, dst[r0:r1, :], src[r0:r1, :], self.wdep[nm]))
        self.conv_sems[l] = sems
        return out

    def emit_conv(self, cv):
        sem, dst, src, dep = cv
        self.mk.dma("pool", sem, dst, src, W=[dep])

    def gcol(self, kindidx, l, c):
        L = len(self.kinds)
        j = (kindidx * L + l) * DC + c
        return self.gains[:, j:j + 1]

    def wtiles(self, nm, idxs=None):
        ntl, E, wt, kc = self.wshape[nm]
        idxs = range(ntl) if idxs is None else idxs
        return [(self.wbf[nm].ap()[i], E, [self.wdep[nm]]) for i in idxs]

    def norm_to_h(self, xt, xdep, h, hdeps, gidx, l, sq, sqd, ss, ssd, rstd, rstdd, ctr):
        mk = self.mk
        for c in range(DC):
            i = ctr[0] % 2
            ctr[0] += 1
            mk.act(sq[i][:], xt[:, c, :], AF.Square, R=[xdep], W=[sqd[i]])
            mk.mm(ss[:], self.oneD, sq[i][:], c == 0, c == DC - 1, R=[sqd[i], self.cdep], W=[ssd])
        mk.act(rstd[:], ss[:], AF.Sqrt, R=[ssd], W=[rstdd], bias=EPS, scale=1.0)
        mk.op("dve", lambda e: e.reciprocal(out=rstd[:], in_=rstd[:]), R=[rstdd], W=[rstdd])
        for c in range(DC):
            mk.stt("dve", h[:, c, :], xt[:, c, :], self.gcol(gidx, l, c), rstd[:], ALU.mult, ALU.mult,
                   R=[xdep, rstdd, self.cdep], W=[hdeps[c]])

    def qk_epilogue(self, ps, psd, t, E, rope, dst_ap, stat_specs, P=128):
        mk = self.mk
        i = E["ctr"] % 2
        E["ctr"] += 1
        outst, outd = E["outst"][i], E["outd"][i]
        if rope is None:
            mk.act(outst[0:P, :], ps[0:P, :], AF.Copy, R=[psd], W=[outd])
        else:
            R_sb, cs, csd = rope
            qbf, qbfd = E["qbf"][i], E["qbfd"][i]
            rot, rotd = E["rot"][i], E["rotd"][i]
            t1, t1d = E["t1"][i], E["t1d"][i]
            t2, t2d = E["t2"][i], E["t2d"][i]
            mk.act(qbf[0:P, :], ps[0:P, :], AF.Copy, R=[psd], W=[qbfd])
            mk.mm(rot[0:P, :], R_sb[0:P, 0:P], qbf[0:P, :], True, True, R=[qbfd, self.cdep], W=[rotd])
            mk.tt("dve", t1[0:P, :], ps[0:P, :], cs[0:P, 0, :], ALU.mult, R=[psd, csd], W=[t1d])
            mk.tt("dve", t2[0:P, :], rot[0:P, :], cs[0:P, 1, :], ALU.mult, R=[rotd, csd], W=[t2d])
            mk.tt("pool", outst[0:P, :], t1[0:P, :], t2[0:P, :], ALU.add, R=[t1d, t2d], W=[outd])
        mk.dma("pool", E["ssem"][i], dst_ap, outst[0:P, :], R=[outd], W=[])
        if stat_specs:
            sqb, sqbd = E["sqb"][i], E["sqbd"][i]
            mk.act(sqb[0:P, :], outst[0:P, :], AF.Square, R=[outd], W=[sqbd])
            for ones_ap, col in stat_specs:
                nb, nbd = E["nb"], E["nbd"]
                red, redd = E["red"], E["redd"]
                mk.mm(nb[:], ones_ap, sqb[0:P, :], True, True, R=[sqbd, self.cdep], W=[nbd])
                if t == 0:
                    mk.op("dve", lambda e, col=col: e.reduce_max(out=self.stats[:, col:col + 1], in_=nb[:], axis=AX.X),
                          R=[nbd], W=[self.stats_dep])
                else:
                    mk.op("dve", lambda e: e.reduce_max(out=red[:], in_=nb[:], axis=AX.X), R=[nbd], W=[redd])
                    mk.tt("dve", self.stats[:, col:col + 1], self.stats[:, col:col + 1], red[:], ALU.max,
                          R=[redd, self.stats_dep], W=[self.stats_dep])

    def epi_bufs(self, st, rope=True):
        mk = self.mk
        E = {"ctr": 0}
        E["outst"] = [mk.sbuf("outst", [128, NT], BF16, st) for _ in range(2)]
        E["outd"] = deps(2)
        E["ssem"] = [mk.dsem(f"st{mk.uid}_{i}") for i in range(2)]
        E["sqb"] = [mk.sbuf("sqb", [128, NT], BF16, st) for _ in range(2)]
        E["sqbd"] = deps(2)
        E["nb"] = mk.psum("nb", [128, NT], F32, st)
        E["nbd"] = PDep()
        E["red"] = mk.sbuf("red", [128, 1], F32, st)
        E["redd"] = Dep()
        if rope:
            E["qbf"] = [mk.sbuf("qbf", [128, NT], BF16, st) for _ in range(2)]
            E["qbfd"] = deps(2)
            E["rot"] = [mk.psum("rot", [128, NT], F32, st) for _ in range(2)]
            E["rotd"] = pdeps(2)
            E["t1"] = [mk.sbuf("t1", [128, NT], F32, st) for _ in range(2)]
            E["t1d"] = deps(2)
            E["t2"] = [mk.sbuf("t2", [128, NT], F32, st) for _ in range(2)]
            E["t2d"] = deps(2)
        return E

    def p1_common_bufs(self, st):
        mk = self.mk
        B = {}
        B["xt"] = [mk.sbuf("xt", [128, DC, NT], F32, st) for _ in range(2)]
        B["xd"] = deps(2)
        B["xsem"] = [mk.dsem(f"xl{mk.uid}_{i}") for i in range(2)]
        B["h"] = mk.sbuf("h", [128, DC, NT], BF16, st)
        B["hd"] = deps(DC)
        B["sq"] = [mk.sbuf("sq", [128, NT], BF16, st) for _ in range(2)]
        B["sqd"] = deps(2)
        B["ss"] = mk.psum("ss", [128, NT], F32, st)
        B["ssd"] = PDep()
        B["rstd"] = mk.sbuf("rstd", [128, NT], F32, st)
        B["rstdd"] = Dep()
        B["ctr"] = [0]
        B["mm"] = [mk.psum("mm", [128, NT], F32, st) for _ in range(2)]
        B["mmd"] = pdeps(2)
        B["mmc"] = 0
        return B

    def load_x(self, B, xin, t):
        i = t % 2
        src = xin.ap().rearrange("c p n -> p c n")[:, :, t * NT:(t + 1) * NT]
        for c0 in range(0, DC, 4):
            self.mk.dma("sp", B["xsem"][i], B["xt"][i][:, c0:c0 + 4, :], src[:, c0:c0 + 4, :], W=[B["xd"][i]])

    def proj_fm(self, B, ws, nchunks_per_tile, wt, KC, rhs_fn, rhs_deps, epilogue, cw=128):
        mk = self.mk
        slot, sd = ws.get()
        for cc in range(nchunks_per_tile):
            i = B["mmc"] % 2
            B["mmc"] += 1
            ps, psd = B["mm"][i], B["mmd"][i]
            for k in range(KC):
                mk.mm(ps[0:cw, :], slot[:, k * wt + cc * cw:k * wt + cc * cw + cw], rhs_fn(k), k == 0, k == KC - 1,
                      R=[sd] + rhs_deps(k), W=[psd])
            epilogue(cc, ps, psd)

    def p1_A(self, l, xin):
        mk = self.mk
        nm = f"l{l}_qkv"
        with contextlib.ExitStack() as st:
            B = self.p1_common_bufs(st)
            E = self.epi_bufs(st, rope=True)
            cs = [mk.sbuf("cs", [128, 2, NT], F32, st) for _ in range(2)]
            csd = deps(2)
            cssem = [mk.dsem(f"cs{mk.uid}_{i}") for i in range(2)]
            vst = [mk.sbuf("vst", [128, 4, 4, 128], BF16, st) for _ in range(2)]
            vstd = deps(2)
            vsem = [mk.dsem(f"vs{mk.uid}_{i}") for i in range(2)]
            vps = [mk.psum("vps", [128, 512], F32, st) for _ in range(2)]
            vpsd = pdeps(2)
            ws = WStream(mk, st)
            for t in range(NTT):
                ws.schedule(self.wtiles(nm))
            self.load_x(B, xin, 0)
            vctr = 0
            for t in range(NTT):
                i = t % 2
                xt, xd = B["xt"][i], B["xd"][i]
                mk.dma("sp", cssem[i], cs[i][:], self.ropeA_d.ap()[:, :, t * NT:(t + 1) * NT], W=[csd[i]])
                ws.prefetch()
                if t + 1 < NTT:
                    self.load_x(B, xin, t + 1)
                self.norm_to_h(xt, xd, B["h"], B["hd"], 0, l, B["sq"], B["sqd"], B["ss"], B["ssd"],
                               B["rstd"], B["rstdd"], B["ctr"])
                h, hd = B["h"], B["hd"]
                tsl = slice(t * NT, (t + 1) * NT)
                for wti in range(10):
                    def epi(cc, ps, psd, wti=wti):
                        c = wti * 2 + cc
                        if c < 16:
                            dst = self.qT.ap()[c][:, tsl]
                            self.qk_epilogue(ps, psd, t, E, (self.RA, cs[i], csd[i]), dst, [(self.one1, c)])
                        else:
                            dst = self.kT.ap()[c - 16][:, tsl]
                            self.qk_epilogue(ps, psd, t, E, (self.RA, cs[i], csd[i]), dst, [(self.one1, 16 + c - 16)])
                    self.proj_fm(B, ws, 2, 256, DC, lambda k: h[:, k, :], lambda k: [hd[k]], epi)
                vi = t % 2
                for wti in range(2):
                    slot, sd = ws.get()
                    for s in range(4):
                        j = vctr % 2
                        vctr += 1
                        for k in range(DC):
                            mk.mm(vps[j][:, 0:256], h[:, k, s * 128:(s + 1) * 128], slot[:, k * 256:(k + 1) * 256],
                                  k == 0, k == DC - 1, R=[sd, hd[k]], W=[vpsd[j]])
                        mk.act(vst[vi][:, 2 * wti:2 * wti + 2, s, :],
                               vps[j][:, 0:256].rearrange("p (a d) -> p a d", a=2), AF.Copy, R=[vpsd[j]], W=[vstd[vi]])
                dst = self.vS.ap()[0:4].rearrange("k p c d -> p k c d")[:, :, 4 * t:4 * t + 4, :]
                mk.dma("pool", vsem[vi], dst, vst[vi][:], R=[vstd[vi]], W=[])

    def attn_consts(self, st, nheads, qcols, kcols, scale, kmap):
        mk = self.mk
        negc = mk.sbuf("negc", [128, nheads], F32, st)
        negcd = Dep()
        for h in range(nheads):
            qc, kc = qcols(h), kcols(kmap(h))
            mk.ts("dve", negc[:, h:h + 1], self.stats[:, qc:qc + 1], self.stats[:, kc:kc + 1], scale * scale,
                  ALU.mult, ALU.mult, R=[self.stats_dep], W=[negcd])
        mk.act(negc[:], negc[:], AF.Sqrt, R=[negcd], W=[negcd])
        mk.act(negc[:], negc[:], AF.Copy, R=[negcd], W=[negcd], scale=-1.0)
        return negc, negcd

    def p2_A(self, l):
        mk = self.mk
        scale = 128 ** -0.5
        with contextlib.ExitStack() as st:
            negc, negcd = self.attn_consts(st, 16, lambda h: h, lambda k: 16 + k, scale, lambda h: h // 4)
            sinks = mk.sbuf("sinks", [128, 16], F32, st)
            sinkd = Dep()
            ssem = mk.dsem(f"sk{mk.uid}")
            mk.dma("sp", ssem, sinks[:], self.aux[l].ap(), W=[sinkd])
            mk.tt("dve", sinks[:], sinks[:], negc[:], ALU.add, R=[sinkd, negcd], W=[sinkd])
            mk.act(sinks[:], sinks[:], AF.Exp, R=[sinkd], W=[sinkd])
            qs = [mk.sbuf("qs", [128, S], BF16, st) for _ in range(2)]
            qd = deps(2)
            qsem = [mk.dsem(f"q{mk.uid}_{i}") for i in range(2)]
            ks = [mk.sbuf("ks", [128, S], BF16, st) for _ in range(2)]
            kd = deps(2)
            ksem = [mk.dsem(f"k{mk.uid}_{i}") for i in range(2)]
            vs = [mk.sbuf("vs", [128, 32, 128], BF16, st) for _ in range(2)]
            vd = deps(2)
            vsem = [mk.dsem(f"v{mk.uid}_{i}") for i in range(2)]
            oh = [mk.sbuf("oh", [128, S], BF16, st) for _ in range(2)]
            ohd = deps(2)
            osem = [mk.dsem(f"o{mk.uid}_{i}") for i in range(2)]
            sps = [mk.psum("sps", [128, 512], F32, st) for _ in range(3)]
            spsd = pdeps(3)
            pb = [mk.sbuf("pb", [128, 512], BF16, st) for _ in range(3)]
            pbd = deps(3)
            ops_ = [mk.psum("ops", [128, 512], F32, st) for _ in range(2)]
            opsd = pdeps(2)
            sums = [mk.psum("sums", [128, 512], F32, st) for _ in range(2)]
            sumsd = pdeps(2)
            den = mk.sbuf("den", [128, 512], F32, st)
            dend = Dep()

            def load_head(h):
                i = h % 2
                mk.dma("sp", qsem[i], qs[i][:], self.qT.ap()[h], W=[qd[i]])
                if h % 4 == 0:
                    j = (h // 4) % 2
                    mk.dma("sp", ksem[j], ks[j][:], self.kT.ap()[h // 4], W=[kd[j]])
                    mk.dma("sp", vsem[j], vs[j][:], self.vS.ap()[h // 4], W=[vd[j]])
            load_head(0)
            gctr = 0
            for h in range(16):
                if h + 1 < 16:
                    load_head(h + 1)
                i = h % 2
                j = (h // 4) % 2
                q_, k_, v_ = qs[i], ks[j], vs[j]
                for g in range(8):
                    gi = gctr % 2
                    gctr += 1
                    lo = [128 if g == 0 else 0, 0, 0]
                    hi = [512, 512, 384 if g == 7 else 512]
                    for jj in range(3):
                        for bb in range(4):
                            b = 4 * g + bb
                            kc = b + jj - 1
                            if kc < 0 or kc > 31:
                                continue
                            mk.mm(sps[jj][:, bb * 128:(bb + 1) * 128], k_[:, kc * 128:(kc + 1) * 128],
                                  q_[:, b * 128:(b + 1) * 128], True, True, R=[kd[j], qd[i]], W=[spsd[jj]])
                        mk.act(pb[jj][:, lo[jj]:hi[jj]], sps[jj][:, lo[jj]:hi[jj]], AF.Exp, R=[spsd[jj], negcd],
                               W=[pbd[jj]], bias=negc[:, h:h + 1], scale=scale)
                        if jj != 1:
                            mi = 0 if jj == 0 else 1
                            mk.tt("pool", pb[jj][:, lo[jj]:hi[jj]], pb[jj][:, lo[jj]:hi[jj]],
                                  self.maskA[:, mi, lo[jj]:hi[jj]], ALU.mult, R=[pbd[jj], self.cdep], W=[pbd[jj]])
                    for bb in range(4):
                        b = 4 * g + bb
                        jjs = [jj for jj in range(3) if 0 <= b + jj - 1 <= 31]
                        for n, jj in enumerate(jjs):
                            kc = b + jj - 1
                            mk.mm(ops_[gi][:, bb * 128:(bb + 1) * 128], v_[:, kc, :], pb[jj][:, bb * 128:(bb + 1) * 128],
                                  n == 0, n == len(jjs) - 1, R=[vd[j], pbd[jj]], W=[opsd[gi]])
                        for n, jj in enumerate(jjs):
                            mk.mm(sums[gi][:, bb * 128:(bb + 1) * 128], self.one1, pb[jj][:, bb * 128:(bb + 1) * 128],
                                  n == 0, n == len(jjs) - 1, R=[pbd[jj], self.cdep], W=[sumsd[gi]])
                    mk.op("dve", lambda e, gi=gi, h=h: e.tensor_scalar_add(out=den[:], in0=sums[gi][:],
                                                                         scalar1=sinks[:, h:h + 1]),
                          R=[sumsd[gi], sinkd], W=[dend])
                    mk.op("dve", lambda e: e.reciprocal(out=den[:], in_=den[:]), R=[dend], W=[dend])
                    mk.tt("dve", oh[i][:, g * 512:(g + 1) * 512], ops_[gi][:], den[:], ALU.mult,
                          R=[opsd[gi], dend], W=[ohd[i]])
                mk.dma("pool", osem[i], self.oT.ap()[h], oh[i][:], R=[ohd[i]], W=[])

    def p1_C(self, l, xin):
        mk = self.mk
        nm = f"l{l}_qkv"
        with contextlib.ExitStack() as st:
            B = self.p1_common_bufs(st)
            E = self.epi_bufs(st, rope=False)
            vst = [mk.sbuf("vst", [128, 16, 4, 128], BF16, st) for _ in range(2)]
            vstd = deps(2)
            vsem = [mk.dsem(f"vs{mk.uid}_{i}") for i in range(2)]
            vps = [mk.psum("vps", [128, 512], F32, st) for _ in range(2)]
            vpsd = pdeps(2)
            ws = WStream(mk, st)
            for t in range(NTT):
                ws.schedule(self.wtiles(nm))
            self.load_x(B, xin, 0)
            vctr = 0
            for t in range(NTT):
                i = t % 2
                xt, xd = B["xt"][i], B["xd"][i]
                ws.prefetch()
                if t + 1 < NTT:
                    self.load_x(B, xin, t + 1)
                self.norm_to_h(xt, xd, B["h"], B["hd"], 0, l, B["sq"], B["sqd"], B["ss"], B["ssd"],
                               B["rstd"], B["rstdd"], B["ctr"])
                h, hd = B["h"], B["hd"]
                tsl = slice(t * NT, (t + 1) * NT)
                for wti in range(16):
                    def epi(cc, ps, psd, wti=wti):
                        c = wti * 2 + cc
                        if c < 16:
                            self.qk_epilogue(ps, psd, t, E, None, self.qT.ap()[c][:, tsl], [(self.one1, c)])
                        else:
                            self.qk_epilogue(ps, psd, t, E, None, self.kT.ap()[c - 16][:, tsl], [(self.one1, c)])
                    self.proj_fm(B, ws, 2, 256, DC, lambda k: h[:, k, :], lambda k: [hd[k]], epi)
                vi = t % 2
                for wti in range(8):
                    slot, sd = ws.get()
                    for s in range(4):
                        j = vctr % 2
                        vctr += 1
                        for k in range(DC):
                            mk.mm(vps[j][:, 0:256], h[:, k, s * 128:(s + 1) * 128], slot[:, k * 256:(k + 1) * 256],
                                  k == 0, k == DC - 1, R=[sd, hd[k]], W=[vpsd[j]])
                        mk.act(vst[vi][:, 2 * wti:2 * wti + 2, s, :],
                               vps[j][:, 0:256].rearrange("p (a d) -> p a d", a=2), AF.Copy, R=[vpsd[j]], W=[vstd[vi]])
                dst = self.vS.ap().rearrange("k p c d -> p k c d")[:, :, 4 * t:4 * t + 4, :]
                for k0 in range(0, 16, 4):
                    mk.dma("pool", vsem[vi], dst[:, k0:k0 + 4], vst[vi][:, k0:k0 + 4], R=[vstd[vi]], W=[])

    def p2_C(self, l):
        mk = self.mk
        scale = 128 ** -0.5
        plan = c_tile_plan()
        with contextlib.ExitStack() as st:
            negc, negcd = self.attn_consts(st, 16, lambda h: h, lambda k: 16 + k, scale, lambda h: h)
            qs = [mk.sbuf("qs", [128, S], BF16, st) for _ in range(2)]
            ks = [mk.sbuf("ks", [128, S], BF16, st) for _ in range(2)]
            vs = [mk.sbuf("vs", [128, 32, 128], BF16, st) for _ in range(2)]
            hd_ = deps(2)
            hsem = [mk.dsem(f"hc{mk.uid}_{i}") for i in range(2)]
            bt = [mk.sbuf("bt", [128, 21 * 128], F32, st) for _ in range(2)]
            btd = deps(2)
            bsem = [mk.dsem(f"bt{mk.uid}_{i}") for i in range(2)]
            et = [mk.sbuf("et", [128, 21 * 128], BF16, st) for _ in range(2)]
            etd = deps(2)
            oh = [mk.sbuf("oh", [128, S], BF16, st) for _ in range(2)]
            ohd = deps(2)
            osem = [mk.dsem(f"o{mk.uid}_{i}") for i in range(2)]
            sps = [mk.psum("sps", [128, 1024], F32, st) for _ in range(2)]
            spsd = pdeps(2)
            pe_ = [mk.sbuf("pe", [128, 640], F32, st) for _ in range(2)]
            ped = deps(2)
            pm = [mk.sbuf("pm", [128, 640], BF16, st) for _ in range(2)]
            pmd = deps(2)
            ops_ = [mk.psum("ops", [128, 512], F32, st) for _ in range(2)]
            opsd = pdeps(2)
            sums = [mk.psum("sums", [128, 512], F32, st) for _ in range(2)]
            sumsd = pdeps(2)
            den = mk.sbuf("den", [128, 512], F32, st)
            dend = Dep()

            def load_head(h):
                i = h % 2
                mk.dma("sp", hsem[i], qs[i][:], self.qT.ap()[h], W=[hd_[i]])
                mk.dma("sp", hsem[i], ks[i][:], self.kT.ap()[h], W=[hd_[i]])
                mk.dma("sp", hsem[i], vs[i][:], self.vS.ap()[h], W=[hd_[i]])
                mk.dma("sp", bsem[i], bt[i][:], self.aux[l].ap()[h], W=[btd[i]])
            load_head(0)
            gctr = 0
            pctr = 0
            for h in range(16):
                i = h % 2
                mk.act(et[i][:], bt[i][:], AF.Exp, R=[btd[i]], W=[etd[i]])
                if h + 1 < 16:
                    load_head(h + 1)
                q_, k_, v_ = qs[i], ks[i], vs[i]
                for g in range(8):
                    gi = gctr % 2
                    gctr += 1
                    for bb in range(4):
                        p = 4 * g + bb
                        kps, case0 = plan[p]
                        n = len(kps)
                        pi = pctr % 2
                        pctr += 1
                        for ii, kp in enumerate(kps):
                            mk.mm(sps[pi][:, ii * 128:(ii + 1) * 128], k_[:, kp * 128:(kp + 1) * 128],
                                  q_[:, p * 128:(p + 1) * 128], True, True, R=[hd_[i]], W=[spsd[pi]])
                        mk.act(pe_[pi][:, 0:n * 128], sps[pi][:, 0:n * 128], AF.Exp, R=[spsd[pi], negcd], W=[ped[pi]],
                               bias=negc[:, h:h + 1], scale=scale)
                        mk.tt("pool", pm[pi][:, 0:n * 128], pe_[pi][:, 0:n * 128],
                              et[i][:, case0 * 128:(case0 + n) * 128], ALU.mult, R=[ped[pi], etd[i]], W=[pmd[pi]])
                        for ii, kp in enumerate(kps):
                            mk.mm(ops_[gi][:, bb * 128:(bb + 1) * 128], v_[:, kp, :], pm[pi][:, ii * 128:(ii + 1) * 128],
                                  ii == 0, ii == n - 1, R=[hd_[i], pmd[pi]], W=[opsd[gi]])
                        for ii, kp in enumerate(kps):
                            mk.mm(sums[gi][:, bb * 128:(bb + 1) * 128], self.one1, pm[pi][:, ii * 128:(ii + 1) * 128],
                                  ii == 0, ii == n - 1, R=[pmd[pi], self.cdep], W=[sumsd[gi]])
                    mk.op("dve", lambda e, gi=gi: e.reciprocal(out=den[:], in_=sums[gi][:]), R=[sumsd[gi]], W=[dend])
                    mk.tt("dve", oh[i][:, g * 512:(g + 1) * 512], ops_[gi][:], den[:], ALU.mult,
                          R=[opsd[gi], dend], W=[ohd[i]])
                mk.dma("pool", osem[i], self.oT.ap()[h], oh[i][:], R=[ohd[i]], W=[])

    def p1_B(self, l, xin):
        mk = self.mk
        with contextlib.ExitStack() as st:
            B = self.p1_common_bufs(st)
            E = self.epi_bufs(st, rope=True)
            cs = [mk.sbuf("cs", [128, 2, NT], F32, st) for _ in range(2)]
            csd = deps(2)
            cssem = [mk.dsem(f"cs{mk.uid}_{i}") for i in range(2)]
            gn = mk.sbuf("gn", [128, 8], F32, st)
            gnd = Dep()
            gsem = mk.dsem(f"gn{mk.uid}")
            mk.dma("sp", gsem, gn[:], self.aux[l].ap(), W=[gnd])
            ct = mk.sbuf("ct", [128, 8, NT], F32, st)
            ctd = deps(8)
            cn = mk.sbuf("cn", [128, 8, NT], BF16, st)
            cnd = deps(8)
            ss2 = B["ss"]
            ss2d = B["ssd"]
            rstd2 = mk.sbuf("rstd2", [128, NT], F32, st)
            rstd2d = Dep()
            vst = [mk.sbuf("vst", [128, 16, 4, 128], BF16, st)] * 2
            vstd = [Dep()] * 2
            vsem = [mk.dsem(f"vs{mk.uid}")] * 2
            vps = [mk.psum("vps", [128, 512], F32, st) for _ in range(2)]
            vpsd = pdeps(2)
            ws = WStream(mk, st, nslot=3)
            for t in range(NTT):
                ws.schedule(self.wtiles(f"l{l}_down"))
                ws.schedule(self.wtiles(f"l{l}_uq"))
                ws.schedule(self.wtiles(f"l{l}_ukv"))
            self.load_x(B, xin, 0)
            vctr = 0
            for t in range(NTT):
                i = t % 2
                xt, xd = B["xt"][i], B["xd"][i]
                mk.dma("sp", cssem[i], cs[i][:], self.ropeB_d.ap()[:, :, t * NT:(t + 1) * NT], W=[csd[i]])
                ws.prefetch()
                if t + 1 < NTT:
                    self.load_x(B, xin, t + 1)
                self.norm_to_h(xt, xd, B["h"], B["hd"], 0, l, B["sq"], B["sqd"], B["ss"], B["ssd"],
                               B["rstd"], B["rstdd"], B["ctr"])
                h, hd = B["h"], B["hd"]
                tsl = slice(t * NT, (t + 1) * NT)
                rope = (self.RB, cs[i], csd[i])
                for c in range(9):
                    def epi(cc, ps, psd, c=c):
                        if c == 8:
                            self.qk_epilogue(ps, psd, t, E, rope, self.kT.ap()[16][0:64, tsl],
                                             [(self.one1[0:64, :], 48)], P=64)
                            return
                        mk.act(ct[:, c, :], ps[:], AF.Copy, R=[psd], W=[ctd[c]])
                        si = B["ctr"][0] % 2
                        B["ctr"][0] += 1
                        mk.act(B["sq"][si][:], ps[:], AF.Square, R=[psd], W=[B["sqd"][si]])
                        mk.mm(ss2[:], self.one512, B["sq"][si][:], c % 4 == 0, c % 4 == 3,
                              R=[B["sqd"][si], self.cdep], W=[ss2d])
                        if c % 4 == 3:
                            mk.act(rstd2[:], ss2[:], AF.Sqrt, R=[ss2d], W=[rstd2d], bias=EPS, scale=1.0)
                            mk.op("dve", lambda e: e.reciprocal(out=rstd2[:], in_=rstd2[:]), R=[rstd2d], W=[rstd2d])
                            for c2 in range(c - 3, c + 1):
                                mk.stt("dve", cn[:, c2, :], ct[:, c2, :], gn[:, c2:c2 + 1], rstd2[:], ALU.mult,
                                       ALU.mult, R=[ctd[c2], rstd2d, gnd], W=[cnd[c2]])
                    self.proj_fm(B, ws, 1, 128, DC, lambda k: h[:, k, :], lambda k: [hd[k]], epi)
                for wti in range(2):
                    def epi(cc, ps, psd, wti=wti):
                        hh = wti * 8 + cc
                        self.qk_epilogue(ps, psd, t, E, None, self.qT.ap()[hh][:, tsl], [(self.one1, hh)])
                    self.proj_fm(B, ws, 8, 1024, 4, lambda k: cn[:, k, :], lambda k: [cnd[k]], epi)

                def epi(cc, ps, psd):
                    self.qk_epilogue(ps, psd, t, E, rope, self.qT.ap()[16 + cc][0:64, tsl],
                                     [(self.one1[0:64, :], 16 + cc)], P=64)
                self.proj_fm(B, ws, 16, 1024, 4, lambda k: cn[:, k, :], lambda k: [cnd[k]], epi, cw=64)
                for wti in range(2):
                    def epi(cc, ps, psd, wti=wti):
                        hh = wti * 8 + cc
                        self.qk_epilogue(ps, psd, t, E, None, self.kT.ap()[hh][:, tsl], [(self.one1, 32 + hh)])
                    self.proj_fm(B, ws, 8, 1024, 4, lambda k: cn[:, 4 + k, :], lambda k: [cnd[4 + k]], epi)
                vi = t % 2
                for wti in range(2):
                    slot, sd = ws.get()
                    for s in range(4):
                        for cg in range(2):
                            j = vctr % 2
                            vctr += 1
                            for k in range(4):
                                mk.mm(vps[j][:], cn[:, 4 + k, s * 128:(s + 1) * 128],
                                      slot[:, k * 1024 + cg * 512:k * 1024 + cg * 512 + 512],
                                      k == 0, k == 3, R=[sd, cnd[4 + k]], W=[vpsd[j]])
                            h0 = wti * 8 + cg * 4
                            mk.act(vst[vi][:, h0:h0 + 4, s, :], vps[j][:].rearrange("p (a d) -> p a d", a=4), AF.Copy,
                                   R=[vpsd[j]], W=[vstd[vi]])
                dst = self.vS.ap().rearrange("k p c d -> p k c d")[:, :, 4 * t:4 * t + 4, :]
                for k0 in range(0, 16, 4):
                    mk.dma("pool", vsem[vi], dst[:, k0:k0 + 4], vst[vi][:, k0:k0 + 4], R=[vstd[vi]], W=[])

    def p2_B(self, l):
        mk = self.mk
        scale = 192 ** -0.5
        with contextlib.ExitStack() as st:
            mk.tt("dve", self.stats[:, 0:16], self.stats[:, 0:16], self.stats[:, 16:32], ALU.add,
                  R=[self.stats_dep], W=[self.stats_dep])
            mk.op("dve", lambda e: e.tensor_scalar_add(out=self.stats[:, 32:48], in0=self.stats[:, 32:48],
                                                       scalar1=self.stats[:, 48:49]),
                  R=[self.stats_dep], W=[self.stats_dep])
            negc, negcd = self.attn_consts(st, 16, lambda h: h, lambda k: 32 + k, scale, lambda h: h)
            qn = [mk.sbuf("qn", [128, S], BF16, st) for _ in range(2)]
            qr = [mk.sbuf("qr", [64, S], BF16, st) for _ in range(2)]
            kn = [mk.sbuf("kn", [128, S], BF16, st) for _ in range(2)]
            vs = [mk.sbuf("vs", [128, 32, 128], BF16, st) for _ in range(2)]
            hd_ = deps(2)
            hsem = [mk.dsem(f"hb{mk.uid}_{i}") for i in range(2)]
            kr = mk.sbuf("kr", [64, S], BF16, st)
            krd = Dep()
            krsem = mk.dsem(f"kr{mk.uid}")
            mk.dma("sp", krsem, kr[:], self.kT.ap()[16][0:64, :], W=[krd])
            oh = [mk.sbuf("oh", [128, S], BF16, st) for _ in range(2)]
            ohd = deps(2)
            osem = [mk.dsem(f"o{mk.uid}_{i}") for i in range(2)]
            sps = [mk.psum("sps", [128, 512], F32, st) for _ in range(4)]
            spsd = pdeps(4)
            pb = [mk.sbuf("pb", [128, 512], BF16, st) for _ in range(4)]
            pbd = deps(4)
            ops_ = [mk.psum("ops", [128, 512], F32, st) for _ in range(2)]
            opsd = pdeps(2)
            sums = [mk.psum("sums", [128, 512], F32, st) for _ in range(2)]
            sumsd = pdeps(2)
            den = mk.sbuf("den", [128, 512], F32, st)
            dend = Dep()

            def load_head(h):
                i = h % 2
                mk.dma("sp", hsem[i], qn[i][:], self.qT.ap()[h], W=[hd_[i]])
                mk.dma("sp", hsem[i], qr[i][:], self.qT.ap()[16 + h][0:64, :], W=[hd_[i]])
                mk.dma("sp", hsem[i], kn[i][:], self.kT.ap()[h], W=[hd_[i]])
                mk.dma("sp", hsem[i], vs[i][:], self.vS.ap()[h], W=[hd_[i]])
            load_head(0)
            gctr = 0
            NKC = 32
            for h in range(16):
                i = h % 2
                if h + 1 < 16:
                    load_head(h + 1)
                for g in range(8):
                    gi = gctr % 2
                    gctr += 1
                    qsl = slice(g * 512, (g + 1) * 512)
                    for j in range(NKC + 2):
                        if j < NKC:
                            b = j % 4
                            ksl = slice(j * 128, (j + 1) * 128)
                            mk.mm(sps[b][:], kn[i][:, ksl], qn[i][:, qsl], True, False, R=[hd_[i]], W=[spsd[b]])
                            mk.mm(sps[b][:], kr[:, ksl], qr[i][:, qsl], False, True, R=[hd_[i], krd], W=[spsd[b]])
                        if 1 <= j <= NKC:
                            b = (j - 1) % 4
                            mk.act(pb[b][:], sps[b][:], AF.Exp, R=[spsd[b], negcd], W=[pbd[b]],
                                   bias=negc[:, h:h + 1], scale=scale)
                        if j >= 2:
                            jj = j - 2
                            b = jj % 4
                            mk.mm(ops_[gi][:], vs[i][:, jj, :], pb[b][:], jj == 0, jj == NKC - 1,
                                  R=[hd_[i], pbd[b]], W=[opsd[gi]])
                            mk.mm(sums[gi][:], self.one1, pb[b][:], jj == 0, jj == NKC - 1,
                                  R=[pbd[b], self.cdep], W=[sumsd[gi]])
                    mk.op("dve", lambda e, gi=gi: e.reciprocal(out=den[:], in_=sums[gi][:]), R=[sumsd[gi]], W=[dend])
                    mk.tt("dve", oh[i][:, qsl], ops_[gi][:], den[:], ALU.mult, R=[opsd[gi], dend], W=[ohd[i]])
                mk.dma("pool", osem[i], self.oT.ap()[h], oh[i][:], R=[ohd[i]], W=[])

    def p3(self, l, xin, convs=()):
        mk = self.mk
        with contextlib.ExitStack() as st:
            xt = mk.sbuf("xt3", [128, DC, NT], F32, st)
            xd = deps(DC)
            xsem = mk.dsem(f"x3{mk.uid}")
            xssem = mk.dsem(f"x3s{mk.uid}")
            mt = mk.sbuf("mt", [128, DC, NT], F32, st)
            md = deps(DC)
            h = mk.sbuf("h3", [128, DC, NT], BF16, st)
            hd = deps(DC)
            actb = mk.sbuf("actb", [128, FC, NT], BF16, st)
            ad = deps(FC)
            osem = mk.dsem(f"o3{mk.uid}")
            sq = [mk.sbuf("sq3", [128, NT], BF16, st) for _ in range(2)]
            sqd = deps(2)
            ss = mk.psum("ss3", [128, NT], F32, st)
            ssd = PDep()
            rstd = mk.sbuf("rstd3", [128, NT], F32, st)
            rstdd = Dep()
            mm = [mk.psum("mm3", [128, NT], F32, st) for _ in range(2)]
            mmd = pdeps(2)
            gps = [mk.psum("gps", [128, NT], F32, st) for _ in range(2)]
            gpsd = pdeps(2)
            ups = [mk.psum("ups", [128, NT], F32, st) for _ in range(2)]
            upsd = pdeps(2)
            gsb = [mk.sbuf("gsb", [128, NT], F32, st) for _ in range(2)]
            gsbd = deps(2)
            ws = WStream(mk, st)
            for t in range(NTT):
                ws.schedule(self.wtiles(f"l{l}_o"))
                ws.schedule(self.wtiles(f"l{l}_win"))
                ws.schedule(self.wtiles(f"l{l}_wout"))
            sqc = [0]
            mmc = [0]

            def rstd_from_ss():
                mk.act(rstd[:], ss[:], AF.Sqrt, R=[ssd], W=[rstdd], bias=EPS, scale=1.0)
                mk.op("dve", lambda e: e.reciprocal(out=rstd[:], in_=rstd[:]), R=[rstdd], W=[rstdd])

            def evac_and_stats(ps, psd, c):
                mk.act(mt[:, c, :], ps[:], AF.Copy, R=[psd], W=[md[c]])
                i = sqc[0] % 2
                sqc[0] += 1
                mk.act(sq[i][:], ps[:], AF.Square, R=[psd], W=[sqd[i]])
                mk.mm(ss[:], self.oneD, sq[i][:], c == 0, c == DC - 1, R=[sqd[i], self.cdep], W=[ssd])

            def residual(gidx):
                rstd_from_ss()
                for c in range(DC):
                    mk.stt("dve", mt[:, c, :], mt[:, c, :], self.gcol(gidx, l, c), rstd[:], ALU.mult, ALU.mult,
                           R=[md[c], rstdd, self.cdep], W=[md[c]])
                    mk.tt("pool", xt[:, c, :], xt[:, c, :], mt[:, c, :], ALU.add, R=[md[c], xd[c]], W=[xd[c]])

            convs = list(convs)
            for t in range(NTT):
                tsl = slice(t * NT, (t + 1) * NT)
                for cv in convs[t::NTT]:
                    self.emit_conv(cv)
                osrc = self.oT.ap().rearrange("h p n -> p h n")[:, :, tsl]
                xsrc = xin.ap().rearrange("c p n -> p c n")[:, :, tsl]
                for c0 in range(0, DC, 4):
                    mk.dma("sp", osem, actb[:, c0:c0 + 4, :], osrc[:, c0:c0 + 4, :], W=ad[c0:c0 + 4])
                for c0 in range(0, DC, 4):
                    mk.dma("sp", xsem, xt[:, c0:c0 + 4, :], xsrc[:, c0:c0 + 4, :], W=xd[c0:c0 + 4])
                ws.prefetch()
                for wti in range(8):
                    slot, sd = ws.get()
                    for cc in range(2):
                        c = wti * 2 + cc
                        i = mmc[0] % 2
                        mmc[0] += 1
                        for k in range(DC):
                            mk.mm(mm[i][:], slot[:, k * 256 + cc * 128:k * 256 + cc * 128 + 128], actb[:, k, :],
                                  k == 0, k == DC - 1, R=[sd, ad[k]], W=[mmd[i]])
                        evac_and_stats(mm[i], mmd[i], c)
                residual(1)
                for c in range(DC):
                    i = sqc[0] % 2
                    sqc[0] += 1
                    mk.act(sq[i][:], xt[:, c, :], AF.Square, R=[xd[c]], W=[sqd[i]])
                    mk.mm(ss[:], self.oneD, sq[i][:], c == 0, c == DC - 1, R=[sqd[i], self.cdep], W=[ssd])
                rstd_from_ss()
                for c in range(DC):
                    mk.stt("dve", h[:, c, :], xt[:, c, :], self.gcol(2, l, c), rstd[:], ALU.mult, ALU.mult,
                           R=[xd[c], rstdd, self.cdep], W=[hd[c]])
                for j in range(FC):
                    slot, sd = ws.get()
                    i = j % 2
                    for k in range(DC):
                        mk.mm(gps[i][:], slot[:, k * 256:k * 256 + 128], h[:, k, :], k == 0, k == DC - 1,
                              R=[sd, hd[k]], W=[gpsd[i]])
                    for k in range(DC):
                        mk.mm(ups[i][:], slot[:, k * 256 + 128:k * 256 + 256], h[:, k, :], k == 0, k == DC - 1,
                              R=[sd, hd[k]], W=[upsd[i]])
                    mk.act(gsb[i][:], gps[i][:], AF.Silu, R=[gpsd[i]], W=[gsbd[i]])
                    mk.tt("dve", actb[:, j, :], gsb[i][:], ups[i][:], ALU.mult, R=[gsbd[i], upsd[i]], W=[ad[j]])
                for c in range(DC):
                    slot, sd = ws.get()
                    i = mmc[0] % 2
                    mmc[0] += 1
                    for j in range(FC):
                        mk.mm(mm[i][:], slot[:, j * 128:(j + 1) * 128], actb[:, j, :], j == 0, j == FC - 1,
                              R=[sd, ad[j]], W=[mmd[i]])
                    evac_and_stats(mm[i], mmd[i], c)
                residual(3)
                ydst = self.yT.ap().rearrange("c p n -> p c n")[:, :, tsl]
                for c0 in range(0, DC, 4):
                    mk.dma("pool", xssem, ydst[:, c0:c0 + 4, :], xt[:, c0:c0 + 4, :], R=xd[c0:c0 + 4], W=[])


def _bf(a):
    return np.ascontiguousarray(a.astype(ml_dtypes.bfloat16))


def host_consts(kinds):
    c = {}
    ones = np.zeros((128, 5 * 128), np.float32)
    ones[:, 0:128] = 1.0
    ones[:, 128:256] = 1.0 / 2048
    ones[:, 256:384] = 1.0 / 512
    ones[0:64, 384:512] = 1.0
    ones[64:128, 512:640] = 1.0
    c["ones"] = _bf(ones)
    if 0 in kinds:
        cosT, sinT, R = rope_tabs(32, 1)
        c["ropeA"] = np.ascontiguousarray(np.stack([cosT, sinT], axis=1))
        c["RA"] = _bf(R)
        jp = np.arange(128)[:, None]
        a = np.arange(128)[None, :]
        mprev = (jp >= a).astype(np.float32)
        mnext = (jp <= a).astype(np.float32)
        m = np.stack([np.tile(mprev, (1, 4)), np.tile(mnext, (1, 4))], axis=1)
        c["maskA"] = _bf(m)
    if 1 in kinds:
        cosT, sinT, R = rope_tabs(64, 2)
        c["ropeB"] = np.ascontiguousarray(np.stack([cosT, sinT], axis=1))
        c["RB"] = _bf(R)
    return c


def host_layer_weights(l, kind, slot, inp, li):
    out = {}
    f = np.float32
    if kind == 0:
        out[f"l{l}_qkv"] = tile_w(inp["a_w_qkv"][slot], 256)
        out[f"l{l}_o"] = tile_w(inp["a_w_o"][slot], 256)
        out[f"l{l}_sinks"] = np.ascontiguousarray(np.broadcast_to(inp["a_sinks"][slot][None, :], (128, 16))).astype(f)
    elif kind == 1:
        wd = inp["b_w_down"][slot]
        wd2 = np.concatenate([wd, wd[:, 1024:1088]], axis=1)
        out[f"l{l}_down"] = tile_w(wd2, 128)
        wuq = inp["b_w_uq"][slot].reshape(512, 16, 192)
        wuq2 = np.concatenate([wuq[:, :, :128].reshape(512, 2048), wuq[:, :, 128:].reshape(512, 1024)], axis=1)
        out[f"l{l}_uq"] = tile_w(wuq2, 1024)
        wukv = inp["b_w_ukv"][slot].reshape(512, 16, 256)
        wukv2 = np.concatenate([wukv[:, :, :128].reshape(512, 2048), wukv[:, :, 128:].reshape(512, 2048)], axis=1)
        out[f"l{l}_ukv"] = tile_w(wukv2, 1024)
        out[f"l{l}_o"] = tile_w(inp["b_w_o"][slot], 256)
        qn = inp["b_q_norm"][slot].reshape(4, 128).T
        kn = inp["b_kv_norm"][slot].reshape(4, 128).T
        out[f"l{l}_qkvn"] = np.ascontiguousarray(np.concatenate([qn, kn], axis=1)).astype(f)
    else:
        out[f"l{l}_qkv"] = tile_w(inp["c_w_qkv"][slot], 256)
        out[f"l{l}_o"] = tile_w(inp["c_w_o"][slot], 256)
        out[f"l{l}_cbias"] = c_bias_tables(inp["c_rel_bias"][slot]).reshape(16, 128, 21 * 128)
    win = inp["ffn_w_in"][li].reshape(2048, 2, FC, 128).transpose(0, 2, 1, 3).reshape(2048, 2 * DFF)
    out[f"l{l}_win"] = tile_w(win, 256)
    out[f"l{l}_wout"] = tile_w(inp["ffn_w_out"][li], 128)
    return out


def host_gains(inp, layers):
    L = len(layers)
    g = np.zeros((128, 4 * L * DC), np.float32)
    for ki, key in enumerate(["pre_mix_norm", "post_mix_norm", "pre_ffn_norm", "post_ffn_norm"]):
        for l, li in enumerate(layers):
            g[:, (ki * L + l) * DC:(ki * L + l + 1) * DC] = inp[key][li].reshape(DC, 128).T
    return g


_PROG_CACHE = {}


def get_prog(kinds):
    key = tuple(kinds)
    if key not in _PROG_CACHE:
        p = Prog(list(kinds))
        nc = p.build()
        _PROG_CACHE[key] = (p, nc)
    return _PROG_CACHE[key]


def run_layers(xT_all, inp, layers, ncores=8):
    kinds = [li % 3 for li in layers]
    p, nc = get_prog(kinds)
    shared = host_consts(kinds)
    shared["gains"] = host_gains(inp, layers)
    for l, li in enumerate(layers):
        shared.update(host_layer_weights(l, li % 3, li // 3, inp, li))
    in_maps = []
    for b in range(ncores):
        m = dict(shared)
        m["xT"] = xT_all[b]
        in_maps.append(m)
    res = run_bass_kernel_spmd(nc, in_maps, core_ids=list(range(ncores)))
    return np.stack([np.asarray(r["yT"]) for r in res.results], axis=0)


LAUNCH_GROUPS = [[0, 1, 2, 3]]


def kernel(**inp):
    inp = {k: np.asarray(v) for k, v in inp.items()}
    x = inp["x"]
    B = x.shape[0]
    xT = np.ascontiguousarray(x.transpose(0, 2, 1)).reshape(B, DC, 128, S)
    for grp in LAUNCH_GROUPS:
        xT = run_layers(xT, inp, grp, ncores=B)
    return np.ascontiguousarray(xT.reshape(B, D, S).transpose(0, 2, 1)).astype(np.float32)
```

```python
import contextlib
import numpy as np
import ml_dtypes
import concourse.bass as bass
import concourse.mybir as mybir
from concourse.bass_utils import run_bass_kernel_spmd

F32 = mybir.dt.float32
BF16 = mybir.dt.bfloat16
ALU = mybir.AluOpType
AF = mybir.ActivationFunctionType
AX = mybir.AxisListType

D = 2048
S = 4096
DFF = 5632
NT = 512
NTT = S // NT
DC = D // 128
FC = DFF // 128
EPS = 1e-6
THETA = 500000.0
SLOT = 5632
NSLOT = 4
NEG = -30000.0
ANNOTATE = False


class Sem:
    def __init__(self, h, name):
        self.h = h
        self.name = name
        self.count = 0


class Dep:
    __slots__ = ("w", "r", "excl")

    def __init__(self, excl=False):
        self.w = None
        self.r = {}
        self.excl = excl


def deps(n):
    return [Dep() for _ in range(n)]


def pdeps(n):
    return [Dep(True) for _ in range(n)]


def PDep():
    return Dep(True)


class Q:
    def __init__(self, name, is_pe=False):
        self.name = name
        self.thunks = []
        self.sem = None
        self.seen = {}
        self.is_pe = is_pe


class MK:
    def __init__(self):
        self.nc = bass.Bass("TRN2", target_bir_lowering=False)
        self.stack = contextlib.ExitStack()
        self.q = {n: Q(n, n == "pe") for n in ("pe", "act", "dve", "pool", "sp")}
        for n in ("pe", "act", "dve", "pool"):
            self.q[n].sem = self.sem("e_" + n)
        self.dma_sems = []
        self.free_sems = []
        self.phase_sems = []
        self.n_inst = 0
        self.uid = 0
        self.phase = "init"

    def sem(self, name):
        h = self.stack.enter_context(self.nc.semaphore(name))
        return Sem(h, name)

    def dsem(self, name, phase=True):
        if self.free_sems:
            s = self.free_sems.pop()
        else:
            s = self.sem(f"d{len(self.dma_sems)}")
            self.dma_sems.append(s)
        if phase:
            self.phase_sems.append(s)
        return s

    def end_phase(self):
        self.barrier()
        self.free_sems.extend(self.phase_sems)
        self.phase_sems = []

    def release(self, sems):
        self.free_sems.extend(sems)

    def sbuf(self, name, shape, dt, stack=None):
        self.uid += 1
        return (stack or self.stack).enter_context(
            self.nc.sbuf_tensor(f"{name}_{self.uid}", list(shape), dt))

    def psum(self, name, shape, dt=F32, stack=None):
        self.uid += 1
        return (stack or self.stack).enter_context(
            self.nc.psum_tensor(f"{name}_{self.uid}", list(shape), dt))

    def dram(self, name, shape, dt, kind="Internal"):
        return self.nc.dram_tensor(name, list(shape), dt, kind=kind)

    def _waits(self, q, R, W):
        need = {}

        def add(ev):
            if ev is None:
                return
            s, v = ev
            if need.get(s, 0) < v:
                need[s] = v
        for d in R:
            add(d.w)
        for d in W:
            add(d.w)
            for s, v in d.r.items():
                add((s, v))
        for s, v in need.items():
            if q.is_pe and s is q.sem:
                continue
            if q.seen.get(s, 0) >= v:
                continue
            q.seen[s] = v
            q.thunks.append(("w", s.h, v))

    def _commit(self, ev, R, W):
        s, v = ev
        for d in W:
            d.w = ev
            d.r = {}
        for d in R:
            if d.r.get(s, 0) < v:
                d.r[s] = v

    def op(self, qn, fn, R=(), W=()):
        q = self.q[qn]
        if any(d.excl for d in R):
            W = list(W) + [d for d in R if d.excl]
            R = [d for d in R if not d.excl]
        self._waits(q, R, W)
        q.sem.count += 1
        ev = (q.sem, q.sem.count)
        q.thunks.append(("i", fn, q.sem.h, 1, self.phase))
        self._commit(ev, R, W)
        self.n_inst += 1
        return ev

    def dma(self, qn, sem, out, in_, R=(), W=(), **kw):
        q = self.q[qn]
        self._waits(q, R, W)
        sem.count += 16
        ev = (sem, sem.count)
        q.thunks.append(("i", lambda e, o=out, i=in_, k=kw: e.dma_start(out=o, in_=i, **k), sem.h, 16, self.phase))
        self._commit(ev, R, W)
        self.n_inst += 1
        return ev

    def barrier(self):
        allsems = [self.q[n].sem for n in ("pe", "act", "dve", "pool")] + self.dma_sems
        for qn, q in self.q.items():
            for s in allsems:
                if s.count == 0:
                    continue
                if q.is_pe and s is q.sem:
                    continue
                if q.seen.get(s, 0) >= s.count:
                    continue
                q.seen[s] = s.count
                q.thunks.append(("w", s.h, s.count))

    def finish(self):
        self.barrier()
        nc = self.nc
        engs = {"pe": "tensor", "act": "scalar", "dve": "vector", "pool": "gpsimd", "sp": "sync"}
        with nc.Block() as block:
            for qn, attr in engs.items():
                q = self.q[qn]

                def body(e, q=q):
                    for t in q.thunks:
                        if t[0] == "w":
                            e.wait_ge(t[1], t[2])
                        else:
                            ins = t[1](e)
                            ins.then_inc(t[2], t[3])
                            if ANNOTATE:
                                ins.annotate(t[4])
                getattr(block, attr)(body)
        self.stack.close()
        return nc

    def mm(self, out, lhsT, rhs, start, stop, R, W):
        return self.op("pe", lambda e: e.matmul(out, lhsT, rhs, start=start, stop=stop), R=R, W=W)

    def act(self, out, in_, func, R, W, bias=None, scale=None):
        kw = {}
        if bias is not None:
            kw["bias"] = bias
        if scale is not None:
            kw["scale"] = scale
        return self.op("act", lambda e: e.activation(out=out, in_=in_, func=func, **kw), R=R, W=W)

    def tt(self, qn, out, in0, in1, op, R, W):
        return self.op(qn, lambda e: e.tensor_tensor(out=out, in0=in0, in1=in1, op=op), R=R, W=W)

    def ts(self, qn, out, in0, s1, s2, op0, op1, R, W):
        return self.op(qn, lambda e: e.tensor_scalar(out=out, in0=in0, scalar1=s1, scalar2=s2, op0=op0, op1=op1),
                       R=R, W=W)

    def stt(self, qn, out, in0, scalar, in1, op0, op1, R, W):
        return self.op(qn, lambda e: e.scalar_tensor_tensor(out=out, in0=in0, scalar=scalar, in1=in1,
                                                            op0=op0, op1=op1), R=R, W=W)


class WStream:
    def __init__(self, mk, stack, nslot=NSLOT):
        self.mk = mk
        self.slots = [mk.sbuf("wslot", [128, SLOT], BF16, stack) for _ in range(nslot)]
        self.deps = deps(nslot)
        self.sems = [mk.dsem(f"ws{mk.uid}_{i}") for i in range(nslot)]
        self.plan = []
        self.issued = 0
        self.taken = 0

    def schedule(self, tiles):
        self.plan.extend(tiles)

    def _issue_upto(self, n):
        n = min(n, len(self.plan))
        while self.issued < n:
            ap, E, rd = self.plan[self.issued]
            s = self.issued % len(self.slots)
            self.mk.dma("sp", self.sems[s], self.slots[s][:, 0:E], ap, R=list(rd), W=[self.deps[s]])
            self.issued += 1

    def get(self):
        i = self.taken
        self._issue_upto(i + len(self.slots))
        self.taken += 1
        s = i % len(self.slots)
        return self.slots[s], self.deps[s]

    def prefetch(self):
        self._issue_upto(self.taken + len(self.slots))


def tile_w(W, wt):
    K, N = W.shape
    KC, NTL = K // 128, N // wt
    return np.ascontiguousarray(W.reshape(KC, 128, NTL, wt).transpose(2, 1, 0, 3)).reshape(NTL, 128, KC * wt)


def rope_tabs(dim, nrep):
    pos = np.arange(S, dtype=np.float32)
    inv = (np.float32(THETA) ** (-np.arange(0, dim, 2, dtype=np.float32) / np.float32(dim))).astype(np.float32)
    ang = (pos[None, :] * inv[:, None]).astype(np.float32)
    c, s = np.cos(ang).astype(np.float32), np.sin(ang).astype(np.float32)
    cosT = np.ones((128, S), np.float32)
    sinT = np.zeros((128, S), np.float32)
    R = np.zeros((128, 128), np.float32)
    half = dim // 2
    for r in range(nrep):
        b = r * dim
        cosT[b:b + half] = c
        cosT[b + half:b + dim] = c
        sinT[b:b + half] = -s
        sinT[b + half:b + dim] = s
        for i in range(half):
            R[b + i + half, b + i] = 1.0
            R[b + i, b + i + half] = 1.0
    return cosT, sinT, R


def c_tile_plan():
    plan = []
    for p in range(32):
        if p == 0:
            plan.append(([0, 1, 2, 3], 5))
        elif p == 1:
            plan.append(([0, 1, 2, 3], 9))
        elif p == 30:
            plan.append(([28, 29, 30, 31], 13))
        elif p == 31:
            plan.append(([28, 29, 30, 31], 17))
        else:
            plan.append(([p - 2, p - 1, p, p + 1, p + 2], 0))
    return plan


def c_bias_tables(rel_bias):
    cases = [(10, 10 + o) for o in (-2, -1, 0, 1, 2)]
    for p in (0, 1):
        cases += [(p, kp) for kp in (0, 1, 2, 3)]
    for p in (30, 31):
        cases += [(p, kp) for kp in (28, 29, 30, 31)]
    kk = np.arange(128)
    kr_l, kc = kk // 64, kk % 64
    out = np.full((16, 128, 21, 128), NEG, np.float32)
    for ci, (p, kp) in enumerate(cases):
        qr = 2 * p + kr_l[None, :]
        qc = kc[None, :]
        kr = 2 * kp + kr_l[:, None]
        kcc = kc[:, None]
        rs = np.clip(qr - 4, 0, 56)
        rowv = (kr >= rs) & (kr < rs + 8)
        cs = np.clip(qc - 8, 0, 48)
        colv = (kcc >= cs) & (kcc < cs + 16)
        valid = rowv & colv
        dr = np.clip(kr - qr + 7, 0, 14)
        dc = np.clip(kcc - qc + 15, 0, 30)
        g = rel_bias[:, dr, dc]
        out[:, :, ci, :] = np.where(valid[None], g, np.float32(NEG))
    return out


class Prog:
    def __init__(self, kinds):
        self.kinds = kinds
        self.mk = MK()
        self.in_names = []
        self.pend = []

    def defer(self, fn, n=1):
        self.pend.append([n, fn])

    def tick(self):
        cur, self.pend = self.pend, []
        keep = []
        for it in cur:
            it[0] -= 1
            if it[0] <= 0:
                it[1]()
            else:
                keep.append(it)
        self.pend = keep + self.pend

    def flush(self):
        while self.pend:
            cur, self.pend = self.pend, []
            for it in cur:
                it[1]()

    def inp(self, name, shape, dt):
        self.in_names.append(name)
        return self.mk.dram(name, shape, dt, kind="ExternalInput")

    @staticmethod
    def wspecs(kind):
        if kind == 0:
            mix = [("qkv", 2048, 3072, 256), ("o", 2048, 2048, 256)]
        elif kind == 1:
            mix = [("down", 2048, 1152, 128), ("uq", 512, 3072, 1024), ("ukv", 512, 4096, 1024),
                   ("o", 2048, 2048, 256)]
        else:
            mix = [("qkv", 2048, 6144, 256), ("o", 2048, 2048, 256)]
        return mix + [("win", 2048, 2 * DFF, 256), ("wout", DFF, 2048, 128)]

    def build(self):
        mk = self.mk
        L = len(self.kinds)
        self.xT = self.inp("xT", [DC, 128, S], F32)
        self.yT = mk.dram("yT", [DC, 128, S], F32, kind="ExternalOutput")
        self.gains_d = self.inp("gains", [128, 4 * L * DC], F32)
        self.ones_d = self.inp("ones", [128, 5 * 128], BF16)
        self.w32, self.wbf, self.wdep, self.wshape = {}, {}, {}, {}
        for l, kind in enumerate(self.kinds):
            for key, K, N, wt in self.wspecs(kind):
                ntl, E = N // wt, (K // 128) * wt
                nm = f"l{l}_{key}"
                self.w32[nm] = self.inp(nm, [ntl, 128, E], F32)
                self.wbf[nm] = mk.dram(nm + "_bf", [ntl, 128, E], BF16)
                self.wdep[nm] = Dep()
                self.wshape[nm] = (ntl, E, wt, K // 128)
        if 0 in self.kinds:
            self.ropeA_d = self.inp("ropeA", [128, 2, S], F32)
            self.RA_d = self.inp("RA", [128, 128], BF16)
            self.maskA_d = self.inp("maskA", [128, 2, 512], BF16)
        if 1 in self.kinds:
            self.ropeB_d = self.inp("ropeB", [128, 2, S], F32)
            self.RB_d = self.inp("RB", [128, 128], BF16)
        self.aux = {}
        for l, kind in enumerate(self.kinds):
            if kind == 0:
                self.aux[l] = self.inp(f"l{l}_sinks", [128, 16], F32)
            elif kind == 1:
                self.aux[l] = self.inp(f"l{l}_qkvn", [128, 8], F32)
            else:
                self.aux[l] = self.inp(f"l{l}_cbias", [16, 128, 21 * 128], F32)
        self.qT = mk.dram("qT_s", [32, 128, S], BF16)
        self.kT = mk.dram("kT_s", [17, 128, S], BF16)
        self.vS = mk.dram("v_s", [16, 128, 32, 128], BF16)
        self.oT = mk.dram("oT_s", [16, 128, S], BF16)

        self.gains = mk.sbuf("gains", [128, 4 * L * DC], F32)
        self.ones = mk.sbuf("ones", [128, 5 * 128], BF16)
        self.cdep = Dep()
        csem = mk.dsem("const")
        mk.dma("sp", csem, self.gains[:], self.gains_d.ap(), W=[self.cdep])
        mk.dma("sp", csem, self.ones[:], self.ones_d.ap(), W=[self.cdep])
        self.one1 = self.ones[:, 0:128]
        self.oneD = self.ones[:, 128:256]
        self.one512 = self.ones[:, 256:384]
        self.oneH = self.ones[:, 384:512]
        self.oneL = self.ones[:, 512:640]
        if 0 in self.kinds:
            self.RA = mk.sbuf("RA", [128, 128], BF16)
            self.maskA = mk.sbuf("maskA", [128, 2, 512], BF16)
            mk.dma("sp", csem, self.RA[:], self.RA_d.ap(), W=[self.cdep])
            mk.dma("sp", csem, self.maskA[:], self.maskA_d.ap(), W=[self.cdep])
        if 1 in self.kinds:
            self.RB = mk.sbuf("RB", [128, 128], BF16)
            mk.dma("sp", csem, self.RB[:], self.RB_d.ap(), W=[self.cdep])
        self.stats = mk.sbuf("stats", [128, 64], F32)
        self.stats_dep = Dep()

        self.conv_sems = {}
        for cv in self.conv_list(0):
            self.emit_conv(cv)

        for l, kind in enumerate(self.kinds):
            xin = self.xT if l == 0 else self.yT
            p1, p2 = [(self.p1_A, self.p2_A), (self.p1_B, self.p2_B), (self.p1_C, self.p2_C)][kind]
            mk.phase = f"L{l}p1"
            p1(l, xin)
            mk.end_phase()
            mk.phase = f"L{l}p2"
            p2(l)
            mk.end_phase()
            mk.phase = f"L{l}p3"
            self.p3(l, xin, self.conv_list(l + 1) if l + 1 < L else [])
            mk.end_phase()
            mk.release(self.conv_sems.pop(l))
        return mk.finish()

    def conv_list(self, l):
        mk = self.mk
        out = []
        sems = []
        for key, K, N, wt in self.wspecs(self.kinds[l]):
            nm = f"l{l}_{key}"
            ntl, E, _, _ = self.wshape[nm]
            r = 2048 if E % 2048 == 0 else 1408
            a = E // r
            src = self.w32[nm].ap().rearrange("t p (a r) -> (t p a) r", r=r)
            dst = self.wbf[nm].ap().rearrange("t p (a r) -> (t p a) r", r=r)
            rows = ntl * 128 * a
            sem = mk.dsem("cv", phase=False)
            sems.append(sem)
            step = 4096
            for r0 in range(0, rows, step):
                r1 = min(rows, r0 + step)
                out.append((sem, dst[r0:r1, :], src[r0:r1, :], self.wdep[nm]))
        self.conv_sems[l] = sems
        return out

    def emit_conv(self, cv):
        sem, dst, src, dep = cv
        self.mk.dma("pool", sem, dst, src, W=[dep])

    def gcol(self, kindidx, l, c):
        L = len(self.kinds)
        j = (kindidx * L + l) * DC + c
        return self.gains[:, j:j + 1]

    def wtiles(self, nm, idxs=None):
        ntl, E, wt, kc = self.wshape[nm]
        idxs = range(ntl) if idxs is None else idxs
        return [(self.wbf[nm].ap()[i], E, [self.wdep[nm]]) for i in idxs]

    def norm_to_h(self, xt, xdep, h, hdeps, gidx, l, sq, sqd, ss, ssd, rstd, rstdd, ctr):
        mk = self.mk
        for c in range(DC):
            i = ctr[0] % 2
            ctr[0] += 1
            mk.act(sq[i][:], xt[:, c, :], AF.Square, R=[xdep], W=[sqd[i]])
            mk.mm(ss[:], self.oneD, sq[i][:], c == 0, c == DC - 1, R=[sqd[i], self.cdep], W=[ssd])
        mk.act(rstd[:], ss[:], AF.Sqrt, R=[ssd], W=[rstdd], bias=EPS, scale=1.0)
        mk.op("dve", lambda e: e.reciprocal(out=rstd[:], in_=rstd[:]), R=[rstdd], W=[rstdd])
        for c in range(DC):
            mk.stt("dve", h[:, c, :], xt[:, c, :], self.gcol(gidx, l, c), rstd[:], ALU.mult, ALU.mult,
                   R=[xdep, rstdd, self.cdep], W=[hdeps[c]])

    def qk_epilogue(self, ps, psd, t, E, rope, dst_ap, stat_specs, P=128):
        mk = self.mk
        i = E["ctr"] % 2
        E["ctr"] += 1
        outst, outd = E["outst"][i], E["outd"][i]
        sqb, sqbd = E["sqb"][i], E["sqbd"][i]

        def stats_stage():
            for ones_ap, col in stat_specs:
                nb, nbd = E["nb"], E["nbd"]
                red, redd = E["red"], E["redd"]
                mk.mm(nb[:], ones_ap, sqb[0:P, :], True, True, R=[sqbd, self.cdep], W=[nbd])
                if t == 0:
                    mk.op("dve", lambda e, col=col: e.reduce_max(out=self.stats[:, col:col + 1], in_=nb[:], axis=AX.X),
                          R=[nbd], W=[self.stats_dep])
                else:
                    mk.op("dve", lambda e: e.reduce_max(out=red[:], in_=nb[:], axis=AX.X), R=[nbd], W=[redd])
                    mk.tt("dve", self.stats[:, col:col + 1], self.stats[:, col:col + 1], red[:], ALU.max,
                          R=[redd, self.stats_dep], W=[self.stats_dep])

        def store_stage():
            mk.dma("pool", E["ssem"][i], dst_ap, outst[0:P, :], R=[outd], W=[])
            if stat_specs:
                mk.act(sqb[0:P, :], outst[0:P, :], AF.Square, R=[outd], W=[sqbd])
                self.defer(stats_stage)

        if rope is None:
            mk.act(outst[0:P, :], ps[0:P, :], AF.Copy, R=[psd], W=[outd])
            store_stage()
            return
        R_sb, cs, csd = rope
        qbf, qbfd = E["qbf"][i], E["qbfd"][i]
        rot, rotd = E["rot"][i], E["rotd"][i]
        t1, t1d = E["t1"][i], E["t1d"][i]
        t2, t2d = E["t2"][i], E["t2d"][i]
        mk.act(qbf[0:P, :], ps[0:P, :], AF.Copy, R=[psd], W=[qbfd])
        mk.tt("dve", t1[0:P, :], ps[0:P, :], cs[0:P, 0, :], ALU.mult, R=[psd, csd], W=[t1d])

        def rope_stage():
            mk.mm(rot[0:P, :], R_sb[0:P, 0:P], qbf[0:P, :], True, True, R=[qbfd, self.cdep], W=[rotd])
            mk.tt("dve", t2[0:P, :], rot[0:P, :], cs[0:P, 1, :], ALU.mult, R=[rotd, csd], W=[t2d])
            mk.tt("pool", outst[0:P, :], t1[0:P, :], t2[0:P, :], ALU.add, R=[t1d, t2d], W=[outd])
            store_stage()
        self.defer(rope_stage)

    def epi_bufs(self, st, rope=True):
        mk = self.mk
        E = {"ctr": 0}
        E["outst"] = [mk.sbuf("outst", [128, NT], BF16, st) for _ in range(2)]
        E["outd"] = deps(2)
        E["ssem"] = [mk.dsem(f"st{mk.uid}_{i}") for i in range(2)]
        E["sqb"] = [mk.sbuf("sqb", [128, NT], BF16, st) for _ in range(2)]
        E["sqbd"] = deps(2)
        E["nb"] = mk.psum("nb", [128, NT], F32, st)
        E["nbd"] = PDep()
        E["red"] = mk.sbuf("red", [128, 1], F32, st)
        E["redd"] = Dep()
        if rope:
            E["qbf"] = [mk.sbuf("qbf", [128, NT], BF16, st) for _ in range(2)]
            E["qbfd"] = deps(2)
            E["rot"] = [mk.psum("rot", [128, NT], F32, st) for _ in range(2)]
            E["rotd"] = pdeps(2)
            E["t1"] = [mk.sbuf("t1", [128, NT], F32, st) for _ in range(2)]
            E["t1d"] = deps(2)
            E["t2"] = [mk.sbuf("t2", [128, NT], F32, st) for _ in range(2)]
            E["t2d"] = deps(2)
        return E

    def p1_common_bufs(self, st):
        mk = self.mk
        B = {}
        B["xt"] = [mk.sbuf("xt", [128, DC, NT], F32, st) for _ in range(2)]
        B["xd"] = deps(2)
        B["xsem"] = [mk.dsem(f"xl{mk.uid}_{i}") for i in range(2)]
        B["h"] = mk.sbuf("h", [128, DC, NT], BF16, st)
        B["hd"] = deps(DC)
        B["sq"] = [mk.sbuf("sq", [128, NT], BF16, st) for _ in range(2)]
        B["sqd"] = deps(2)
        B["ss"] = mk.psum("ss", [128, NT], F32, st)
        B["ssd"] = PDep()
        B["rstd"] = mk.sbuf("rstd", [128, NT], F32, st)
        B["rstdd"] = Dep()
        B["ctr"] = [0]
        B["mm"] = [mk.psum("mm", [128, NT], F32, st) for _ in range(2)]
        B["mmd"] = pdeps(2)
        B["mmc"] = 0
        return B

    def load_x(self, B, xin, t):
        i = t % 2
        src = xin.ap().rearrange("c p n -> p c n")[:, :, t * NT:(t + 1) * NT]
        for c0 in range(0, DC, 4):
            self.mk.dma("sp", B["xsem"][i], B["xt"][i][:, c0:c0 + 4, :], src[:, c0:c0 + 4, :], W=[B["xd"][i]])

    def proj_fm(self, B, ws, nchunks_per_tile, wt, KC, rhs_fn, rhs_deps, epilogue, cw=128):
        mk = self.mk
        slot, sd = ws.get()
        for cc in range(nchunks_per_tile):
            i = B["mmc"] % 2
            B["mmc"] += 1
            ps, psd = B["mm"][i], B["mmd"][i]
            for k in range(KC):
                mk.mm(ps[0:cw, :], slot[:, k * wt + cc * cw:k * wt + cc * cw + cw], rhs_fn(k), k == 0, k == KC - 1,
                      R=[sd] + rhs_deps(k), W=[psd])
            self.tick()
            epilogue(cc, ps, psd)

    def p1_A(self, l, xin):
        mk = self.mk
        nm = f"l{l}_qkv"
        with contextlib.ExitStack() as st:
            B = self.p1_common_bufs(st)
            E = self.epi_bufs(st, rope=True)
            cs = [mk.sbuf("cs", [128, 2, NT], F32, st) for _ in range(2)]
            csd = deps(2)
            cssem = [mk.dsem(f"cs{mk.uid}_{i}") for i in range(2)]
            vst = [mk.sbuf("vst", [128, 4, 4, 128], BF16, st) for _ in range(2)]
            vstd = deps(2)
            vsem = [mk.dsem(f"vs{mk.uid}_{i}") for i in range(2)]
            vps = [mk.psum("vps", [128, 512], F32, st) for _ in range(2)]
            vpsd = pdeps(2)
            ws = WStream(mk, st)
            for t in range(NTT):
                ws.schedule(self.wtiles(nm))
            self.load_x(B, xin, 0)
            vctr = 0
            for t in range(NTT):
                i = t % 2
                xt, xd = B["xt"][i], B["xd"][i]
                mk.dma("sp", cssem[i], cs[i][:], self.ropeA_d.ap()[:, :, t * NT:(t + 1) * NT], W=[csd[i]])
                ws.prefetch()
                if t + 1 < NTT:
                    self.load_x(B, xin, t + 1)
                self.norm_to_h(xt, xd, B["h"], B["hd"], 0, l, B["sq"], B["sqd"], B["ss"], B["ssd"],
                               B["rstd"], B["rstdd"], B["ctr"])
                h, hd = B["h"], B["hd"]
                tsl = slice(t * NT, (t + 1) * NT)
                for wti in range(10):
                    def epi(cc, ps, psd, wti=wti):
                        c = wti * 2 + cc
                        if c < 16:
                            dst = self.qT.ap()[c][:, tsl]
                            self.qk_epilogue(ps, psd, t, E, (self.RA, cs[i], csd[i]), dst, [(self.one1, c)])
                        else:
                            dst = self.kT.ap()[c - 16][:, tsl]
                            self.qk_epilogue(ps, psd, t, E, (self.RA, cs[i], csd[i]), dst, [(self.one1, 16 + c - 16)])
                    self.proj_fm(B, ws, 2, 256, DC, lambda k: h[:, k, :], lambda k: [hd[k]], epi)
                vi = t % 2
                for wti in range(2):
                    slot, sd = ws.get()
                    for s in range(4):
                        j = vctr % 2
                        vctr += 1
                        for k in range(DC):
                            mk.mm(vps[j][:, 0:256], h[:, k, s * 128:(s + 1) * 128], slot[:, k * 256:(k + 1) * 256],
                                  k == 0, k == DC - 1, R=[sd, hd[k]], W=[vpsd[j]])
                        mk.act(vst[vi][:, 2 * wti:2 * wti + 2, s, :],
                               vps[j][:, 0:256].rearrange("p (a d) -> p a d", a=2), AF.Copy, R=[vpsd[j]], W=[vstd[vi]])
                dst = self.vS.ap()[0:4].rearrange("k p c d -> p k c d")[:, :, 4 * t:4 * t + 4, :]
                mk.dma("pool", vsem[vi], dst, vst[vi][:], R=[vstd[vi]], W=[])
                self.flush()

    def attn_consts(self, st, nheads, qcols, kcols, scale, kmap):
        mk = self.mk
        negc = mk.sbuf("negc", [128, nheads], F32, st)
        negcd = Dep()
        for h in range(nheads):
            qc, kc = qcols(h), kcols(kmap(h))
            mk.ts("dve", negc[:, h:h + 1], self.stats[:, qc:qc + 1], self.stats[:, kc:kc + 1], scale * scale,
                  ALU.mult, ALU.mult, R=[self.stats_dep], W=[negcd])
        mk.act(negc[:], negc[:], AF.Sqrt, R=[negcd], W=[negcd])
        mk.act(negc[:], negc[:], AF.Copy, R=[negcd], W=[negcd], scale=-1.0)
        return negc, negcd

    def p2_A(self, l):
        mk = self.mk
        scale = 128 ** -0.5
        with contextlib.ExitStack() as st:
            negc, negcd = self.attn_consts(st, 16, lambda h: h, lambda k: 16 + k, scale, lambda h: h // 4)
            sinks = mk.sbuf("sinks", [128, 16], F32, st)
            sinkd = Dep()
            ssem = mk.dsem(f"sk{mk.uid}")
            mk.dma("sp", ssem, sinks[:], self.aux[l].ap(), W=[sinkd])
            mk.tt("dve", sinks[:], sinks[:], negc[:], ALU.add, R=[sinkd, negcd], W=[sinkd])
            mk.act(sinks[:], sinks[:], AF.Exp, R=[sinkd], W=[sinkd])
            qs = [mk.sbuf("qs", [128, S], BF16, st) for _ in range(2)]
            qd = deps(2)
            qsem = [mk.dsem(f"q{mk.uid}_{i}") for i in range(2)]
            ks = [mk.sbuf("ks", [128, S], BF16, st) for _ in range(2)]
            kd = deps(2)
            ksem = [mk.dsem(f"k{mk.uid}_{i}") for i in range(2)]
            vs = [mk.sbuf("vs", [128, 32, 128], BF16, st) for _ in range(2)]
            vd = deps(2)
            vsem = [mk.dsem(f"v{mk.uid}_{i}") for i in range(2)]
            oh = [mk.sbuf("oh", [128, S], BF16, st) for _ in range(2)]
            ohd = deps(2)
            osem = [mk.dsem(f"o{mk.uid}_{i}") for i in range(2)]
            sps = [mk.psum("sps", [128, 512], F32, st) for _ in range(3)]
            spsd = pdeps(3)
            pb = [mk.sbuf("pb", [128, 512], BF16, st) for _ in range(3)]
            pbd = deps(3)
            ops_ = [mk.psum("ops", [128, 512], F32, st) for _ in range(2)]
            opsd = pdeps(2)
            sums = [mk.psum("sums", [128, 512], F32, st) for _ in range(2)]
            sumsd = pdeps(2)
            den = mk.sbuf("den", [128, 512], F32, st)
            dend = Dep()

            def load_head(h):
                i = h % 2
                mk.dma("sp", qsem[i], qs[i][:], self.qT.ap()[h], W=[qd[i]])
                if h % 4 == 0:
                    j = (h // 4) % 2
                    mk.dma("sp", ksem[j], ks[j][:], self.kT.ap()[h // 4], W=[kd[j]])
                    mk.dma("sp", vsem[j], vs[j][:], self.vS.ap()[h // 4], W=[vd[j]])
            load_head(0)
            gctr = 0
            for h in range(16):
                if h + 1 < 16:
                    load_head(h + 1)
                i = h % 2
                j = (h // 4) % 2
                q_, k_, v_ = qs[i], ks[j], vs[j]
                for g in range(8):
                    gi = gctr % 2
                    gctr += 1
                    lo = [128 if g == 0 else 0, 0, 0]
                    hi = [512, 512, 384 if g == 7 else 512]
                    for jj in range(3):
                        for bb in range(4):
                            b = 4 * g + bb
                            kc = b + jj - 1
                            if kc < 0 or kc > 31:
                                continue
                            mk.mm(sps[jj][:, bb * 128:(bb + 1) * 128], k_[:, kc * 128:(kc + 1) * 128],
                                  q_[:, b * 128:(b + 1) * 128], True, True, R=[kd[j], qd[i]], W=[spsd[jj]])
                        mk.act(pb[jj][:, lo[jj]:hi[jj]], sps[jj][:, lo[jj]:hi[jj]], AF.Exp, R=[spsd[jj], negcd],
                               W=[pbd[jj]], bias=negc[:, h:h + 1], scale=scale)
                        if jj != 1:
                            mi = 0 if jj == 0 else 1
                            mk.tt("pool", pb[jj][:, lo[jj]:hi[jj]], pb[jj][:, lo[jj]:hi[jj]],
                                  self.maskA[:, mi, lo[jj]:hi[jj]], ALU.mult, R=[pbd[jj], self.cdep], W=[pbd[jj]])
                    for bb in range(4):
                        b = 4 * g + bb
                        jjs = [jj for jj in range(3) if 0 <= b + jj - 1 <= 31]
                        for n, jj in enumerate(jjs):
                            kc = b + jj - 1
                            mk.mm(ops_[gi][:, bb * 128:(bb + 1) * 128], v_[:, kc, :], pb[jj][:, bb * 128:(bb + 1) * 128],
                                  n == 0, n == len(jjs) - 1, R=[vd[j], pbd[jj]], W=[opsd[gi]])
                        for n, jj in enumerate(jjs):
                            mk.mm(sums[gi][:, bb * 128:(bb + 1) * 128], self.one1, pb[jj][:, bb * 128:(bb + 1) * 128],
                                  n == 0, n == len(jjs) - 1, R=[pbd[jj], self.cdep], W=[sumsd[gi]])
                    mk.op("dve", lambda e, gi=gi, h=h: e.tensor_scalar_add(out=den[:], in0=sums[gi][:],
                                                                         scalar1=sinks[:, h:h + 1]),
                          R=[sumsd[gi], sinkd], W=[dend])
                    mk.op("dve", lambda e: e.reciprocal(out=den[:], in_=den[:]), R=[dend], W=[dend])
                    mk.tt("dve", oh[i][:, g * 512:(g + 1) * 512], ops_[gi][:], den[:], ALU.mult,
                          R=[opsd[gi], dend], W=[ohd[i]])
                mk.dma("pool", osem[i], self.oT.ap()[h], oh[i][:], R=[ohd[i]], W=[])

    def p1_C(self, l, xin):
        mk = self.mk
        nm = f"l{l}_qkv"
        with contextlib.ExitStack() as st:
            B = self.p1_common_bufs(st)
            E = self.epi_bufs(st, rope=False)
            vst = [mk.sbuf("vst", [128, 16, 4, 128], BF16, st) for _ in range(2)]
            vstd = deps(2)
            vsem = [mk.dsem(f"vs{mk.uid}_{i}") for i in range(2)]
            vps = [mk.psum("vps", [128, 512], F32, st) for _ in range(2)]
            vpsd = pdeps(2)
            ws = WStream(mk, st)
            for t in range(NTT):
                ws.schedule(self.wtiles(nm))
            self.load_x(B, xin, 0)
            vctr = 0
            for t in range(NTT):
                i = t % 2
                xt, xd = B["xt"][i], B["xd"][i]
                ws.prefetch()
                if t + 1 < NTT:
                    self.load_x(B, xin, t + 1)
                self.norm_to_h(xt, xd, B["h"], B["hd"], 0, l, B["sq"], B["sqd"], B["ss"], B["ssd"],
                               B["rstd"], B["rstdd"], B["ctr"])
                h, hd = B["h"], B["hd"]
                tsl = slice(t * NT, (t + 1) * NT)
                for wti in range(16):
                    def epi(cc, ps, psd, wti=wti):
                        c = wti * 2 + cc
                        if c < 16:
                            self.qk_epilogue(ps, psd, t, E, None, self.qT.ap()[c][:, tsl], [(self.one1, c)])
                        else:
                            self.qk_epilogue(ps, psd, t, E, None, self.kT.ap()[c - 16][:, tsl], [(self.one1, c)])
                    self.proj_fm(B, ws, 2, 256, DC, lambda k: h[:, k, :], lambda k: [hd[k]], epi)
                vi = t % 2
                for wti in range(8):
                    slot, sd = ws.get()
                    for s in range(4):
                        j = vctr % 2
                        vctr += 1
                        for k in range(DC):
                            mk.mm(vps[j][:, 0:256], h[:, k, s * 128:(s + 1) * 128], slot[:, k * 256:(k + 1) * 256],
                                  k == 0, k == DC - 1, R=[sd, hd[k]], W=[vpsd[j]])
                        mk.act(vst[vi][:, 2 * wti:2 * wti + 2, s, :],
                               vps[j][:, 0:256].rearrange("p (a d) -> p a d", a=2), AF.Copy, R=[vpsd[j]], W=[vstd[vi]])
                dst = self.vS.ap().rearrange("k p c d -> p k c d")[:, :, 4 * t:4 * t + 4, :]
                for k0 in range(0, 16, 4):
                    mk.dma("pool", vsem[vi], dst[:, k0:k0 + 4], vst[vi][:, k0:k0 + 4], R=[vstd[vi]], W=[])
                self.flush()

    def p2_C(self, l):
        mk = self.mk
        scale = 128 ** -0.5
        plan = c_tile_plan()
        with contextlib.ExitStack() as st:
            negc, negcd = self.attn_consts(st, 16, lambda h: h, lambda k: 16 + k, scale, lambda h: h)
            qs = [mk.sbuf("qs", [128, S], BF16, st) for _ in range(2)]
            ks = [mk.sbuf("ks", [128, S], BF16, st) for _ in range(2)]
            vs = [mk.sbuf("vs", [128, 32, 128], BF16, st) for _ in range(2)]
            hd_ = deps(2)
            hsem = [mk.dsem(f"hc{mk.uid}_{i}") for i in range(2)]
            bt = [mk.sbuf("bt", [128, 21 * 128], F32, st) for _ in range(2)]
            btd = deps(2)
            bsem = [mk.dsem(f"bt{mk.uid}_{i}") for i in range(2)]
            et = [mk.sbuf("et", [128, 21 * 128], BF16, st) for _ in range(2)]
            etd = deps(2)
            oh = [mk.sbuf("oh", [128, S], BF16, st) for _ in range(2)]
            ohd = deps(2)
            osem = [mk.dsem(f"o{mk.uid}_{i}") for i in range(2)]
            sps = [mk.psum("sps", [128, 1024], F32, st) for _ in range(2)]
            spsd = pdeps(2)
            pe_ = [mk.sbuf("pe", [128, 640], F32, st) for _ in range(2)]
            ped = deps(2)
            pm = [mk.sbuf("pm", [128, 640], BF16, st) for _ in range(2)]
            pmd = deps(2)
            ops_ = [mk.psum("ops", [128, 512], F32, st) for _ in range(2)]
            opsd = pdeps(2)
            sums = [mk.psum("sums", [128, 512], F32, st) for _ in range(2)]
            sumsd = pdeps(2)
            den = mk.sbuf("den", [128, 512], F32, st)
            dend = Dep()

            def load_head(h):
                i = h % 2
                mk.dma("sp", hsem[i], qs[i][:], self.qT.ap()[h], W=[hd_[i]])
                mk.dma("sp", hsem[i], ks[i][:], self.kT.ap()[h], W=[hd_[i]])
                mk.dma("sp", hsem[i], vs[i][:], self.vS.ap()[h], W=[hd_[i]])
                mk.dma("sp", bsem[i], bt[i][:], self.aux[l].ap()[h], W=[btd[i]])
            load_head(0)
            pctr = 0
            for h in range(16):
                i = h % 2
                mk.act(et[i][:], bt[i][:], AF.Exp, R=[btd[i]], W=[etd[i]])
                if h + 1 < 16:
                    load_head(h + 1)
                q_, k_, v_ = qs[i], ks[i], vs[i]

                def front(p, pi):
                    kps, case0 = plan[p]
                    n = len(kps)
                    for ii, kp in enumerate(kps):
                        mk.mm(sps[pi][:, ii * 128:(ii + 1) * 128], k_[:, kp * 128:(kp + 1) * 128],
                              q_[:, p * 128:(p + 1) * 128], True, True, R=[hd_[i]], W=[spsd[pi]])
                    mk.act(pe_[pi][:, 0:n * 128], sps[pi][:, 0:n * 128], AF.Exp, R=[spsd[pi], negcd], W=[ped[pi]],
                           bias=negc[:, h:h + 1], scale=scale)
                    mk.tt("pool" if p % 2 == 0 else "dve", pm[pi][:, 0:n * 128], pe_[pi][:, 0:n * 128],
                          et[i][:, case0 * 128:(case0 + n) * 128], ALU.mult, R=[ped[pi], etd[i]], W=[pmd[pi]])

                def back(p, pi):
                    kps, case0 = plan[p]
                    n = len(kps)
                    g, bb = p // 4, p % 4
                    gi = (h * 8 + g) % 2
                    for ii, kp in enumerate(kps):
                        mk.mm(ops_[gi][:, bb * 128:(bb + 1) * 128], v_[:, kp, :], pm[pi][:, ii * 128:(ii + 1) * 128],
                              ii == 0, ii == n - 1, R=[hd_[i], pmd[pi]], W=[opsd[gi]])
                    for ii, kp in enumerate(kps):
                        mk.mm(sums[gi][:, bb * 128:(bb + 1) * 128], self.one1, pm[pi][:, ii * 128:(ii + 1) * 128],
                              ii == 0, ii == n - 1, R=[pmd[pi], self.cdep], W=[sumsd[gi]])
                    if bb == 3:
                        mk.op("dve", lambda e, gi=gi: e.reciprocal(out=den[:], in_=sums[gi][:]), R=[sumsd[gi]], W=[dend])
                        mk.tt("dve", oh[i][:, g * 512:(g + 1) * 512], ops_[gi][:], den[:], ALU.mult,
                              R=[opsd[gi], dend], W=[ohd[i]])

                prev = None
                for p in range(32):
                    pi = pctr % 2
                    pctr += 1
                    front(p, pi)
                    if prev is not None:
                        back(*prev)
                    prev = (p, pi)
                back(*prev)
                mk.dma("pool", osem[i], self.oT.ap()[h], oh[i][:], R=[ohd[i]], W=[])

    def p1_B(self, l, xin):
        mk = self.mk
        with contextlib.ExitStack() as st:
            B = self.p1_common_bufs(st)
            E = self.epi_bufs(st, rope=True)
            cs = [mk.sbuf("cs", [128, 2, NT], F32, st) for _ in range(2)]
            csd = deps(2)
            cssem = [mk.dsem(f"cs{mk.uid}_{i}") for i in range(2)]
            gn = mk.sbuf("gn", [128, 8], F32, st)
            gnd = Dep()
            gsem = mk.dsem(f"gn{mk.uid}")
            mk.dma("sp", gsem, gn[:], self.aux[l].ap(), W=[gnd])
            ct = mk.sbuf("ct", [128, 8, NT], F32, st)
            ctd = deps(8)
            cn = mk.sbuf("cn", [128, 8, NT], BF16, st)
            cnd = deps(8)
            ss2 = B["ss"]
            ss2d = B["ssd"]
            rstd2 = mk.sbuf("rstd2", [128, NT], F32, st)
            rstd2d = Dep()
            vst = [mk.sbuf("vst", [128, 16, 4, 128], BF16, st)] * 2
            vstd = [Dep()] * 2
            vsem = [mk.dsem(f"vs{mk.uid}")] * 2
            vps = [mk.psum("vps", [128, 512], F32, st) for _ in range(2)]
            vpsd = pdeps(2)
            ws = WStream(mk, st, nslot=3)
            for t in range(NTT):
                ws.schedule(self.wtiles(f"l{l}_down"))
                ws.schedule(self.wtiles(f"l{l}_uq"))
                ws.schedule(self.wtiles(f"l{l}_ukv"))
            self.load_x(B, xin, 0)
            vctr = 0
            for t in range(NTT):
                i = t % 2
                xt, xd = B["xt"][i], B["xd"][i]
                mk.dma("sp", cssem[i], cs[i][:], self.ropeB_d.ap()[:, :, t * NT:(t + 1) * NT], W=[csd[i]])
                ws.prefetch()
                if t + 1 < NTT:
                    self.load_x(B, xin, t + 1)
                self.norm_to_h(xt, xd, B["h"], B["hd"], 0, l, B["sq"], B["sqd"], B["ss"], B["ssd"],
                               B["rstd"], B["rstdd"], B["ctr"])
                h, hd = B["h"], B["hd"]
                tsl = slice(t * NT, (t + 1) * NT)
                rope = (self.RB, cs[i], csd[i])
                for c in range(9):
                    def epi(cc, ps, psd, c=c):
                        if c == 8:
                            self.qk_epilogue(ps, psd, t, E, rope, self.kT.ap()[16][0:64, tsl],
                                             [(self.one1[0:64, :], 48)], P=64)
                            return
                        mk.act(ct[:, c, :], ps[:], AF.Copy, R=[psd], W=[ctd[c]])
                        si = B["ctr"][0] % 2
                        B["ctr"][0] += 1
                        mk.act(B["sq"][si][:], ps[:], AF.Square, R=[psd], W=[B["sqd"][si]])

                        def st2(c=c, si=si):
                            mk.mm(ss2[:], self.one512, B["sq"][si][:], c % 4 == 0, c % 4 == 3,
                                  R=[B["sqd"][si], self.cdep], W=[ss2d])
                            if c % 4 == 3:
                                mk.act(rstd2[:], ss2[:], AF.Sqrt, R=[ss2d], W=[rstd2d], bias=EPS, scale=1.0)
                                mk.op("dve", lambda e: e.reciprocal(out=rstd2[:], in_=rstd2[:]), R=[rstd2d], W=[rstd2d])
                                for c2 in range(c - 3, c + 1):
                                    mk.stt("dve", cn[:, c2, :], ct[:, c2, :], gn[:, c2:c2 + 1], rstd2[:], ALU.mult,
                                           ALU.mult, R=[ctd[c2], rstd2d, gnd], W=[cnd[c2]])
                        self.defer(st2)
                    self.proj_fm(B, ws, 1, 128, DC, lambda k: h[:, k, :], lambda k: [hd[k]], epi)
                self.flush()
                for wti in range(2):
                    def epi(cc, ps, psd, wti=wti):
                        hh = wti * 8 + cc
                        self.qk_epilogue(ps, psd, t, E, None, self.qT.ap()[hh][:, tsl], [(self.one1, hh)])
                    self.proj_fm(B, ws, 8, 1024, 4, lambda k: cn[:, k, :], lambda k: [cnd[k]], epi)

                def epi(cc, ps, psd):
                    self.qk_epilogue(ps, psd, t, E, rope, self.qT.ap()[16 + cc][0:64, tsl],
                                     [(self.one1[0:64, :], 16 + cc)], P=64)
                self.proj_fm(B, ws, 16, 1024, 4, lambda k: cn[:, k, :], lambda k: [cnd[k]], epi, cw=64)
                for wti in range(2):
                    def epi(cc, ps, psd, wti=wti):
                        hh = wti * 8 + cc
                        self.qk_epilogue(ps, psd, t, E, None, self.kT.ap()[hh][:, tsl], [(self.one1, 32 + hh)])
                    self.proj_fm(B, ws, 8, 1024, 4, lambda k: cn[:, 4 + k, :], lambda k: [cnd[4 + k]], epi)
                vi = t % 2
                for wti in range(2):
                    slot, sd = ws.get()
                    for s in range(4):
                        for cg in range(2):
                            j = vctr % 2
                            vctr += 1
                            for k in range(4):
                                mk.mm(vps[j][:], cn[:, 4 + k, s * 128:(s + 1) * 128],
                                      slot[:, k * 1024 + cg * 512:k * 1024 + cg * 512 + 512],
                                      k == 0, k == 3, R=[sd, cnd[4 + k]], W=[vpsd[j]])
                            h0 = wti * 8 + cg * 4
                            mk.act(vst[vi][:, h0:h0 + 4, s, :], vps[j][:].rearrange("p (a d) -> p a d", a=4), AF.Copy,
                                   R=[vpsd[j]], W=[vstd[vi]])
                dst = self.vS.ap().rearrange("k p c d -> p k c d")[:, :, 4 * t:4 * t + 4, :]
                for k0 in range(0, 16, 4):
                    mk.dma("pool", vsem[vi], dst[:, k0:k0 + 4], vst[vi][:, k0:k0 + 4], R=[vstd[vi]], W=[])
                self.flush()

    def p2_B(self, l):
        mk = self.mk
        scale = 192 ** -0.5
        with contextlib.ExitStack() as st:
            mk.tt("dve", self.stats[:, 0:16], self.stats[:, 0:16], self.stats[:, 16:32], ALU.add,
                  R=[self.stats_dep], W=[self.stats_dep])
            mk.op("dve", lambda e: e.tensor_scalar_add(out=self.stats[:, 32:48], in0=self.stats[:, 32:48],
                                                       scalar1=self.stats[:, 48:49]),
                  R=[self.stats_dep], W=[self.stats_dep])
            negc, negcd = self.attn_consts(st, 16, lambda h: h, lambda k: 32 + k, scale, lambda h: h)
            qn = [mk.sbuf("qn", [128, S], BF16, st) for _ in range(2)]
            qr = [mk.sbuf("qr", [64, S], BF16, st) for _ in range(2)]
            kn = [mk.sbuf("kn", [128, S], BF16, st) for _ in range(2)]
            vs = [mk.sbuf("vs", [128, 32, 128], BF16, st) for _ in range(2)]
            hd_ = deps(2)
            hsem = [mk.dsem(f"hb{mk.uid}_{i}") for i in range(2)]
            kr = mk.sbuf("kr", [64, S], BF16, st)
            krd = Dep()
            krsem = mk.dsem(f"kr{mk.uid}")
            mk.dma("sp", krsem, kr[:], self.kT.ap()[16][0:64, :], W=[krd])
            oh = [mk.sbuf("oh", [128, S], BF16, st) for _ in range(2)]
            ohd = deps(2)
            osem = [mk.dsem(f"o{mk.uid}_{i}") for i in range(2)]
            sps = [mk.psum("sps", [128, 512], F32, st) for _ in range(4)]
            spsd = pdeps(4)
            pb = [mk.sbuf("pb", [128, 512], BF16, st) for _ in range(4)]
            pbd = deps(4)
            ops_ = [mk.psum("ops", [128, 512], F32, st) for _ in range(2)]
            opsd = pdeps(2)
            sums = [mk.psum("sums", [128, 512], F32, st) for _ in range(2)]
            sumsd = pdeps(2)
            den = mk.sbuf("den", [128, 512], F32, st)
            dend = Dep()

            def load_head(h):
                i = h % 2
                mk.dma("sp", hsem[i], qn[i][:], self.qT.ap()[h], W=[hd_[i]])
                mk.dma("sp", hsem[i], qr[i][:], self.qT.ap()[16 + h][0:64, :], W=[hd_[i]])
                mk.dma("sp", hsem[i], kn[i][:], self.kT.ap()[h], W=[hd_[i]])
                mk.dma("sp", hsem[i], vs[i][:], self.vS.ap()[h], W=[hd_[i]])
            load_head(0)
            gctr = 0
            NKC = 32
            for h in range(16):
                i = h % 2
                if h + 1 < 16:
                    load_head(h + 1)
                for g in range(8):
                    gi = gctr % 2
                    gctr += 1
                    qsl = slice(g * 512, (g + 1) * 512)
                    for j in range(NKC + 2):
                        if j < NKC:
                            b = j % 4
                            ksl = slice(j * 128, (j + 1) * 128)
                            mk.mm(sps[b][:], kn[i][:, ksl], qn[i][:, qsl], True, False, R=[hd_[i]], W=[spsd[b]])
                            mk.mm(sps[b][:], kr[:, ksl], qr[i][:, qsl], False, True, R=[hd_[i], krd], W=[spsd[b]])
                        if 1 <= j <= NKC:
                            b = (j - 1) % 4
                            mk.act(pb[b][:], sps[b][:], AF.Exp, R=[spsd[b], negcd], W=[pbd[b]],
                                   bias=negc[:, h:h + 1], scale=scale)
                        if j >= 2:
                            jj = j - 2
                            b = jj % 4
                            mk.mm(ops_[gi][:], vs[i][:, jj, :], pb[b][:], jj == 0, jj == NKC - 1,
                                  R=[hd_[i], pbd[b]], W=[opsd[gi]])
                            mk.mm(sums[gi][:], self.one1, pb[b][:], jj == 0, jj == NKC - 1,
                                  R=[pbd[b], self.cdep], W=[sumsd[gi]])
                    mk.op("dve", lambda e, gi=gi: e.reciprocal(out=den[:], in_=sums[gi][:]), R=[sumsd[gi]], W=[dend])
                    mk.tt("dve", oh[i][:, qsl], ops_[gi][:], den[:], ALU.mult, R=[opsd[gi], dend], W=[ohd[i]])
                mk.dma("pool", osem[i], self.oT.ap()[h], oh[i][:], R=[ohd[i]], W=[])

    def p3(self, l, xin, convs=()):
        mk = self.mk
        with contextlib.ExitStack() as st:
            xt = mk.sbuf("xt3", [128, DC, NT], F32, st)
            xd = deps(DC)
            xsem = mk.dsem(f"x3{mk.uid}")
            xssem = mk.dsem(f"x3s{mk.uid}")
            mt = mk.sbuf("mt", [128, DC, NT], F32, st)
            md = deps(DC)
            h = mk.sbuf("h3", [128, DC, NT], BF16, st)
            hd = deps(DC)
            actb = mk.sbuf("actb", [128, FC, NT], BF16, st)
            ad = deps(FC)
            osem = mk.dsem(f"o3{mk.uid}")
            sq = [mk.sbuf("sq3", [128, NT], BF16, st) for _ in range(2)]
            sqd = deps(2)
            ss = mk.psum("ss3", [128, NT], F32, st)
            ssd = PDep()
            rstd = mk.sbuf("rstd3", [128, NT], F32, st)
            rstdd = Dep()
            mm = [mk.psum("mm3", [128, NT], F32, st) for _ in range(2)]
            mmd = pdeps(2)
            gps = [mk.psum("gps", [128, NT], F32, st) for _ in range(2)]
            gpsd = pdeps(2)
            ups = [mk.psum("ups", [128, NT], F32, st) for _ in range(2)]
            upsd = pdeps(2)
            gsb = [mk.sbuf("gsb", [128, NT], F32, st) for _ in range(2)]
            gsbd = deps(2)
            ws = WStream(mk, st)
            for t in range(NTT):
                ws.schedule(self.wtiles(f"l{l}_o"))
                ws.schedule(self.wtiles(f"l{l}_win"))
                ws.schedule(self.wtiles(f"l{l}_wout"))
            sqc = [0]
            mmc = [0]

            def rstd_from_ss():
                mk.act(rstd[:], ss[:], AF.Sqrt, R=[ssd], W=[rstdd], bias=EPS, scale=1.0)
                mk.op("dve", lambda e: e.reciprocal(out=rstd[:], in_=rstd[:]), R=[rstdd], W=[rstdd])

            def evac_and_stats(ps, psd, c):
                mk.act(mt[:, c, :], ps[:], AF.Copy, R=[psd], W=[md[c]])
                i = sqc[0] % 2
                sqc[0] += 1
                mk.act(sq[i][:], ps[:], AF.Square, R=[psd], W=[sqd[i]])
                self.defer(lambda: mk.mm(ss[:], self.oneD, sq[i][:], c == 0, c == DC - 1,
                                         R=[sqd[i], self.cdep], W=[ssd]))

            def residual(gidx):
                self.flush()
                rstd_from_ss()
                for c in range(DC):
                    mk.stt("dve", mt[:, c, :], mt[:, c, :], self.gcol(gidx, l, c), rstd[:], ALU.mult, ALU.mult,
                           R=[md[c], rstdd, self.cdep], W=[md[c]])
                    mk.tt("pool" if c % 2 == 0 else "dve", xt[:, c, :], xt[:, c, :], mt[:, c, :], ALU.add,
                          R=[md[c], xd[c]], W=[xd[c]])

            convs = list(convs)

            def load_ot(t):
                osrc = self.oT.ap().rearrange("h p n -> p h n")[:, :, t * NT:(t + 1) * NT]
                for c0 in range(0, DC, 4):
                    mk.dma("sp", osem, h[:, c0:c0 + 4, :], osrc[:, c0:c0 + 4, :], W=hd[c0:c0 + 4])

            for t in range(NTT):
                tsl = slice(t * NT, (t + 1) * NT)
                for cv in convs[t::NTT]:
                    self.emit_conv(cv)
                xsrc = xin.ap().rearrange("c p n -> p c n")[:, :, tsl]
                if t == 0:
                    load_ot(0)
                for c0 in range(0, DC, 4):
                    mk.dma("sp", xsem, xt[:, c0:c0 + 4, :], xsrc[:, c0:c0 + 4, :], W=xd[c0:c0 + 4])
                ws.prefetch()
                for wti in range(8):
                    slot, sd = ws.get()
                    for cc in range(2):
                        c = wti * 2 + cc
                        i = mmc[0] % 2
                        mmc[0] += 1
                        for k in range(DC):
                            mk.mm(mm[i][:], slot[:, k * 256 + cc * 128:k * 256 + cc * 128 + 128], h[:, k, :],
                                  k == 0, k == DC - 1, R=[sd, hd[k]], W=[mmd[i]])
                        self.tick()
                        evac_and_stats(mm[i], mmd[i], c)
                residual(1)
                for c in range(DC):
                    i = sqc[0] % 2
                    sqc[0] += 1
                    mk.act(sq[i][:], xt[:, c, :], AF.Square, R=[xd[c]], W=[sqd[i]])
                    mk.mm(ss[:], self.oneD, sq[i][:], c == 0, c == DC - 1, R=[sqd[i], self.cdep], W=[ssd])
                rstd_from_ss()
                for c in range(DC):
                    mk.stt("dve", h[:, c, :], xt[:, c, :], self.gcol(2, l, c), rstd[:], ALU.mult, ALU.mult,
                           R=[xd[c], rstdd, self.cdep], W=[hd[c]])
                for j in range(FC):
                    slot, sd = ws.get()
                    i = j % 2
                    for k in range(DC):
                        mk.mm(gps[i][:], slot[:, k * 256:k * 256 + 128], h[:, k, :], k == 0, k == DC - 1,
                              R=[sd, hd[k]], W=[gpsd[i]])
                    for k in range(DC):
                        mk.mm(ups[i][:], slot[:, k * 256 + 128:k * 256 + 256], h[:, k, :], k == 0, k == DC - 1,
                              R=[sd, hd[k]], W=[upsd[i]])
                    mk.act(gsb[i][:], gps[i][:], AF.Silu, R=[gpsd[i]], W=[gsbd[i]])
                    mk.tt("dve", actb[:, j, :], gsb[i][:], ups[i][:], ALU.mult, R=[gsbd[i], upsd[i]], W=[ad[j]])
                if t + 1 < NTT:
                    load_ot(t + 1)
                for c in range(DC):
                    slot, sd = ws.get()
                    i = mmc[0] % 2
                    mmc[0] += 1
                    for j in range(FC):
                        mk.mm(mm[i][:], slot[:, j * 128:(j + 1) * 128], actb[:, j, :], j == 0, j == FC - 1,
                              R=[sd, ad[j]], W=[mmd[i]])
                    self.tick()
                    evac_and_stats(mm[i], mmd[i], c)
                residual(3)
                ydst = self.yT.ap().rearrange("c p n -> p c n")[:, :, tsl]
                for c0 in range(0, DC, 4):
                    mk.dma("pool", xssem, ydst[:, c0:c0 + 4, :], xt[:, c0:c0 + 4, :], R=xd[c0:c0 + 4], W=[])


def _bf(a):
    return np.ascontiguousarray(a.astype(ml_dtypes.bfloat16))


def host_consts(kinds):
    c = {}
    ones = np.zeros((128, 5 * 128), np.float32)
    ones[:, 0:128] = 1.0
    ones[:, 128:256] = 1.0 / 2048
    ones[:, 256:384] = 1.0 / 512
    ones[0:64, 384:512] = 1.0
    ones[64:128, 512:640] = 1.0
    c["ones"] = _bf(ones)
    if 0 in kinds:
        cosT, sinT, R = rope_tabs(32, 1)
        c["ropeA"] = np.ascontiguousarray(np.stack([cosT, sinT], axis=1))
        c["RA"] = _bf(R)
        jp = np.arange(128)[:, None]
        a = np.arange(128)[None, :]
        mprev = (jp >= a).astype(np.float32)
        mnext = (jp <= a).astype(np.float32)
        m = np.stack([np.tile(mprev, (1, 4)), np.tile(mnext, (1, 4))], axis=1)
        c["maskA"] = _bf(m)
    if 1 in kinds:
        cosT, sinT, R = rope_tabs(64, 2)
        c["ropeB"] = np.ascontiguousarray(np.stack([cosT, sinT], axis=1))
        c["RB"] = _bf(R)
    return c


def host_layer_weights(l, kind, slot, inp, li):
    out = {}
    f = np.float32
    if kind == 0:
        out[f"l{l}_qkv"] = tile_w(inp["a_w_qkv"][slot], 256)
        out[f"l{l}_o"] = tile_w(inp["a_w_o"][slot], 256)
        out[f"l{l}_sinks"] = np.ascontiguousarray(np.broadcast_to(inp["a_sinks"][slot][None, :], (128, 16))).astype(f)
    elif kind == 1:
        wd = inp["b_w_down"][slot]
        wd2 = np.concatenate([wd, wd[:, 1024:1088]], axis=1)
        out[f"l{l}_down"] = tile_w(wd2, 128)
        wuq = inp["b_w_uq"][slot].reshape(512, 16, 192)
        wuq2 = np.concatenate([wuq[:, :, :128].reshape(512, 2048), wuq[:, :, 128:].reshape(512, 1024)], axis=1)
        out[f"l{l}_uq"] = tile_w(wuq2, 1024)
        wukv = inp["b_w_ukv"][slot].reshape(512, 16, 256)
        wukv2 = np.concatenate([wukv[:, :, :128].reshape(512, 2048), wukv[:, :, 128:].reshape(512, 2048)], axis=1)
        out[f"l{l}_ukv"] = tile_w(wukv2, 1024)
        out[f"l{l}_o"] = tile_w(inp["b_w_o"][slot], 256)
        qn = inp["b_q_norm"][slot].reshape(4, 128).T
        kn = inp["b_kv_norm"][slot].reshape(4, 128).T
        out[f"l{l}_qkvn"] = np.ascontiguousarray(np.concatenate([qn, kn], axis=1)).astype(f)
    else:
        out[f"l{l}_qkv"] = tile_w(inp["c_w_qkv"][slot], 256)
        out[f"l{l}_o"] = tile_w(inp["c_w_o"][slot], 256)
        out[f"l{l}_cbias"] = c_bias_tables(inp["c_rel_bias"][slot]).reshape(16, 128, 21 * 128)
    win = inp["ffn_w_in"][li].reshape(2048, 2, FC, 128).transpose(0, 2, 1, 3).reshape(2048, 2 * DFF)
    out[f"l{l}_win"] = tile_w(win, 256)
    out[f"l{l}_wout"] = tile_w(inp["ffn_w_out"][li], 128)
    return out


def host_gains(inp, layers):
    L = len(layers)
    g = np.zeros((128, 4 * L * DC), np.float32)
    for ki, key in enumerate(["pre_mix_norm", "post_mix_norm", "pre_ffn_norm", "post_ffn_norm"]):
        for l, li in enumerate(layers):
            g[:, (ki * L + l) * DC:(ki * L + l + 1) * DC] = inp[key][li].reshape(DC, 128).T
    return g


_PROG_CACHE = {}


def get_prog(kinds):
    key = tuple(kinds)
    if key not in _PROG_CACHE:
        p = Prog(list(kinds))
        nc = p.build()
        _PROG_CACHE[key] = (p, nc)
    return _PROG_CACHE[key]


def run_layers(xT_all, inp, layers, ncores=8):
    kinds = [li % 3 for li in layers]
    p, nc = get_prog(kinds)
    shared = host_consts(kinds)
    shared["gains"] = host_gains(inp, layers)
    for l, li in enumerate(layers):
        shared.update(host_layer_weights(l, li % 3, li // 3, inp, li))
    in_maps = []
    for b in range(ncores):
        m = dict(shared)
        m["xT"] = xT_all[b]
        in_maps.append(m)
    res = run_bass_kernel_spmd(nc, in_maps, core_ids=list(range(ncores)))
    return np.stack([np.asarray(r["yT"]) for r in res.results], axis=0)


LAUNCH_GROUPS = [[0, 1, 2, 3]]


def kernel(**inp):
    inp = {k: np.asarray(v) for k, v in inp.items()}
    x = inp["x"]
    B = x.shape[0]
    xT = np.ascontiguousarray(x.transpose(0, 2, 1)).reshape(B, DC, 128, S)
    for grp in LAUNCH_GROUPS:
        xT = run_layers(xT, inp, grp, ncores=B)
    return np.ascontiguousarray(xT.reshape(B, D, S).transpose(0, 2, 1)).astype(np.float32)
```
